# Optimizing a Trainium2 kernel written in Bass

```python
import jax, jax.numpy as jnp
from jax import lax
import numpy as np

D_MODEL = 1024
BATCH = 4
SEQ = 4096
DEPTH = 2
DEC_BATCH = 16
DEC_SEQ = 16
PAST_LEN = 4096

CHUNK = 64
N_META = 16
A_HEADS = 8
A_HEAD_DIM = 64
A_WIDTH = A_HEADS * A_HEAD_DIM
A_DECAY_LORA = 64
A_AAA_LORA = 64
A_GATE_LORA = 128
A_COLS = 3 * A_WIDTH + A_DECAY_LORA + A_AAA_LORA + A_GATE_LORA
B_HEADS = 4
B_HEAD_DIM = 128
B_WIDTH = B_HEADS * B_HEAD_DIM
B_CONV = 4
B_COLS = 4 * B_WIDTH + 2 * B_HEADS
D_MIX = A_WIDTH + B_WIDTH
D_IN = A_COLS + B_COLS
D_FF = 2816
FFN_CONV = 3
RMS_EPS = 1e-6
GN_EPS = 64e-5

kernel_name = "hymba_rwkv7_mlstm_convffn_stream_step"


def rmsnorm(x, g):
    xf = x.astype(jnp.float32)
    y = xf * lax.rsqrt(jnp.mean(xf * xf, axis=-1, keepdims=True) + RMS_EPS)
    return (y * g.astype(jnp.float32)).astype(x.dtype)


def causal_dwconv(x, buf, w, b):
    K = w.shape[0]
    T = x.shape[1]
    xp = jnp.concatenate([buf.astype(x.dtype), x], axis=1)
    y = sum(w[j] * xp[:, j:j + T] for j in range(K)) + b
    return y, xp[:, T:]


def rwkv7_mix(p, shift_prev, S0, mu, w0, w2, a0, a2, g2, k_k, k_a, r_k, ln_w, ln_b):
    Bn, T, _ = p.shape
    prev = jnp.concatenate([shift_prev[:, None].astype(p.dtype), p[:, :-1]], axis=1)
    pm = (p + (prev - p) * mu).astype(jnp.float32)
    i1, i2, i3 = A_WIDTH, 2 * A_WIDTH, 3 * A_WIDTH
    i4, i5 = i3 + A_DECAY_LORA, i3 + A_DECAY_LORA + A_AAA_LORA
    r, k, v = pm[..., :i1], pm[..., i1:i2], pm[..., i2:i3]
    wl, al, gl = pm[..., i3:i4], pm[..., i4:i5], pm[..., i5:]
    w = -jax.nn.softplus(-(w0 + jnp.tanh(wl) @ w2)) - 0.5
    decay = jnp.exp(-jnp.exp(w))
    a = jax.nn.sigmoid(a0 + al @ a2)
    g = jax.nn.sigmoid(gl) @ g2
    hd = lambda t: t.reshape(Bn, T, A_HEADS, A_HEAD_DIM)
    r, k, v, decay, a = hd(r), hd(k), hd(v), hd(decay), hd(a)
    kk = k * k_k.reshape(A_HEADS, A_HEAD_DIM)
    kk = kk * lax.rsqrt(jnp.sum(kk * kk, axis=-1, keepdims=True) + 1e-12)
    k = k * (1.0 + (a - 1.0) * k_a.reshape(A_HEADS, A_HEAD_DIM))

    def step(S, inp):
        r_t, k_t, v_t, w_t, kk_t, a_t = inp
        sa = jnp.einsum('bhvk,bhk->bhv', S, -kk_t)
        S = (S * w_t[:, :, None, :] + sa[..., None] * (kk_t * a_t)[:, :, None, :]
             + v_t[..., None] * k_t[:, :, None, :])
        return S, jnp.einsum('bhvk,bhk->bhv', S, r_t)

    xs = tuple(jnp.moveaxis(t, 1, 0) for t in (r, k, v, decay, kk, a))
    S_new, y = lax.scan(step, S0.astype(jnp.float32), xs)
    y = jnp.moveaxis(y, 0, 1)
    mean = jnp.mean(y, axis=-1, keepdims=True)
    var = jnp.mean(jnp.square(y - mean), axis=-1, keepdims=True)
    y = ((y - mean) * lax.rsqrt(var + GN_EPS)).reshape(Bn, T, A_WIDTH) * ln_w + ln_b
    bonus = jnp.sum(r * k * r_k.reshape(A_HEADS, A_HEAD_DIM), axis=-1, keepdims=True) * v
    y = (y + bonus.reshape(Bn, T, A_WIDTH)) * g
    return y, p[:, -1], S_new


def mlstm_block(state, blk):
    C, n, m = state
    q, k, v, li, lf = blk
    L = q.shape[2]
    b = jnp.cumsum(lf, axis=-1)
    causal = jnp.tril(jnp.ones((L, L), dtype=bool))
    Dm = jnp.where(causal, b[..., :, None] - b[..., None, :] + li[..., None, :], -jnp.inf)
    inter = b + m[..., None]
    mt = jnp.maximum(inter, jnp.max(Dm, axis=-1))
    wi = jnp.exp(Dm - mt[..., None])
    wo = jnp.exp(inter - mt)
    s = jnp.einsum('bhtd,bhjd->bhtj', q, k) * wi
    num = wo[..., None] * jnp.einsum('bhtd,bhde->bhte', q, C) + jnp.einsum('bhtj,bhje->bhte', s, v)
    den = wo * jnp.einsum('bhtd,bhd->bht', q, n) + jnp.sum(s, axis=-1)
    h = num / jnp.maximum(jnp.abs(den), jnp.exp(-mt))[..., None]
    m_new = mt[..., -1]
    ws = jnp.exp(b[..., -1:] - b + li - m_new[..., None])
    dec = jnp.exp(b[..., -1] + m - m_new)
    C_new = dec[..., None, None] * C + jnp.einsum('bhj,bhjd,bhje->bhde', ws, k, v)
    n_new = dec[..., None] * n + jnp.einsum('bhj,bhjd->bhd', ws, k)
    return (C_new, n_new, m_new), h


def mlstm_seq(state, q, k, v, li, lf, lead):
    Bn, H, T, d = q.shape
    state, h0 = mlstm_block(state, (q[:, :, :lead], k[:, :, :lead], v[:, :, :lead], li[:, :, :lead], lf[:, :, :lead]))
    rest = T - lead
    if rest == 0:
        return state, h0
    nb = rest // CHUNK

    def blocks(t):
        t = t[:, :, lead:]
        t = t.reshape(t.shape[:2] + (nb, CHUNK) + t.shape[3:])
        return jnp.moveaxis(t, 2, 0)

    state, hs = lax.scan(mlstm_block, state, (blocks(q), blocks(k), blocks(v), blocks(li), blocks(lf)))
    hs = jnp.moveaxis(hs, 0, 2).reshape(Bn, H, rest, d)
    return state, jnp.concatenate([h0, hs], axis=2)


def mlstm_mix(p, conv_buf, C0, n0, m0, conv_w, conv_b, i_bias, f_bias, hn_w, lead):
    Bn, T, _ = p.shape
    qk, conv_new = causal_dwconv(p[..., :2 * B_WIDTH], conv_buf, conv_w, conv_b)
    qk = jax.nn.silu(qk.astype(jnp.float32))
    pf = p.astype(jnp.float32)
    v = pf[..., 2 * B_WIDTH:3 * B_WIDTH]
    o = pf[..., 3 * B_WIDTH:4 * B_WIDTH]
    gates = pf[..., 4 * B_WIDTH:]
    li = gates[..., :B_HEADS] + i_bias
    lf = jax.nn.log_sigmoid(gates[..., B_HEADS:] + f_bias)
    hd = lambda t: jnp.moveaxis(t.reshape(Bn, T, B_HEADS, B_HEAD_DIM), 2, 1)
    q = hd(qk[..., :B_WIDTH])
    k = hd(qk[..., B_WIDTH:]) * (B_HEAD_DIM ** -0.5)
    v = hd(v)
    li, lf = jnp.moveaxis(li, 2, 1), jnp.moveaxis(lf, 2, 1)
    (C, n, m), h = mlstm_seq((C0.astype(jnp.float32), n0.astype(jnp.float32), m0.astype(jnp.float32)),
                             q, k, v, li, lf, lead)
    h = jnp.moveaxis(h, 1, 2)
    h = h * lax.rsqrt(jnp.mean(h * h, axis=-1, keepdims=True) + RMS_EPS)
    h = h.reshape(Bn, T, B_WIDTH) * hn_w * jax.nn.sigmoid(o)
    return h, conv_new, C, n, m


def layer(x, st, w, lead):
    shift0, S0, bconv0, C0, n0, m0, fconv0 = st
    (norm_mix, w_in, a_mu, a_w0, a_w2, a_a0, a_a2, a_g2, a_k_k, a_k_a, a_r_k, a_ln_w, a_ln_b,
     b_conv_w, b_conv_b, b_i_bias, b_f_bias, b_hn_w, w_out, norm_ffn, w_up, ffn_conv_w, ffn_conv_b, w_down) = w
    h = rmsnorm(x, norm_mix)
    p = h @ w_in
    ya, shift1, S1 = rwkv7_mix(p[..., :A_COLS], shift0, S0, a_mu, a_w0, a_w2, a_a0, a_a2, a_g2,
                               a_k_k, a_k_a, a_r_k, a_ln_w, a_ln_b)
    yb, bconv1, C1, n1, m1 = mlstm_mix(p[..., A_COLS:], bconv0, C0, n0, m0, b_conv_w, b_conv_b,
                                       b_i_bias, b_f_bias, b_hn_w, lead)
    x = x + jnp.concatenate([ya, yb], axis=-1).astype(x.dtype) @ w_out
    h2 = rmsnorm(x, norm_ffn)
    ug = h2 @ w_up
    u, fconv1 = causal_dwconv(ug[..., :D_FF], fconv0, ffn_conv_w, ffn_conv_b)
    x = x + (jax.nn.silu(u) * ug[..., D_FF:]) @ w_down
    return x, (shift1, S1, bconv1, C1, n1, m1, fconv1)


def zero_states(b, dtype):
    f32 = jnp.float32
    return (jnp.zeros((b, A_COLS), dtype), jnp.zeros((b, A_HEADS, A_HEAD_DIM, A_HEAD_DIM), f32),
            jnp.zeros((b, B_CONV - 1, 2 * B_WIDTH), dtype), jnp.zeros((b, B_HEADS, B_HEAD_DIM, B_HEAD_DIM), f32),
            jnp.zeros((b, B_HEADS, B_HEAD_DIM), f32), jnp.zeros((b, B_HEADS), f32),
            jnp.zeros((b, FFN_CONV - 1, D_FF), dtype))


def trunk(x, states, layer_weights, lead):
    outs = []
    for l in range(DEPTH):
        st = zero_states(x.shape[0], x.dtype) if states is None else tuple(s[l] for s in states)
        x, new_st = layer(x, st, tuple(t[l] for t in layer_weights), lead)
        outs.append(new_st)
    return x, tuple(jnp.stack([o[i] for o in outs]) for i in range(7))


def setup_inputs(seed: int = 0) -> dict:
    key = jax.random.key(seed)
    ks = iter(jax.random.split(key, 40))
    nrm = lambda shape, s=1.0: s * jax.random.normal(next(ks), shape, jnp.float32)
    uni = lambda shape, lo, hi: jax.random.uniform(next(ks), shape, jnp.float32, lo, hi)
    L = DEPTH
    return {
        "x_prompt": nrm((BATCH, SEQ, D_MODEL)),
        "x_sample": nrm((DEC_BATCH, DEC_SEQ, D_MODEL)),
        "state_rwkv_shift": nrm((L, DEC_BATCH, A_COLS)),
        "state_rwkv_wkv": nrm((L, DEC_BATCH, A_HEADS, A_HEAD_DIM, A_HEAD_DIM), 0.3),
        "state_mlstm_conv": nrm((L, DEC_BATCH, B_CONV - 1, 2 * B_WIDTH)),
        "state_mlstm_C": nrm((L, DEC_BATCH, B_HEADS, B_HEAD_DIM, B_HEAD_DIM), 0.1),
        "state_mlstm_n": nrm((L, DEC_BATCH, B_HEADS, B_HEAD_DIM), 0.3),
        "state_mlstm_m": nrm((L, DEC_BATCH, B_HEADS), 1.0),
        "state_ffn_conv": nrm((L, DEC_BATCH, FFN_CONV - 1, D_FF)),
        "meta_tokens": nrm((N_META, D_MODEL)),
        "norm_mix": 1.0 + nrm((L, D_MODEL), 0.02),
        "w_in": nrm((L, D_MODEL, D_IN), D_MODEL ** -0.5),
        "a_mu": uni((L, A_COLS), 0.0, 1.0),
        "a_w0": uni((L, A_WIDTH), -6.0, -1.0),
        "a_w2": nrm((L, A_DECAY_LORA, A_WIDTH), 0.1),
        "a_a0": nrm((L, A_WIDTH), 0.5),
        "a_a2": nrm((L, A_AAA_LORA, A_WIDTH), A_AAA_LORA ** -0.5),
        "a_g2": nrm((L, A_GATE_LORA, A_WIDTH), A_GATE_LORA ** -0.5),
        "a_k_k": 0.85 + nrm((L, A_WIDTH), 0.05),
        "a_k_a": 1.0 + nrm((L, A_WIDTH), 0.05),
        "a_r_k": nrm((L, A_WIDTH), 0.1),
        "a_ln_w": 1.0 + nrm((L, A_WIDTH), 0.02),
        "a_ln_b": nrm((L, A_WIDTH), 0.02),
        "b_conv_w": nrm((L, B_CONV, 2 * B_WIDTH), B_CONV ** -0.5),
        "b_conv_b": nrm((L, 2 * B_WIDTH), 0.02),
        "b_i_bias": nrm((L, B_HEADS), 0.1),
        "b_f_bias": 3.0 + nrm((L, B_HEADS), 0.5),
        "b_hn_w": 1.0 + nrm((L, B_WIDTH), 0.02),
        "w_out": nrm((L, D_MIX, D_MODEL), D_MIX ** -0.5),
        "norm_ffn": 1.0 + nrm((L, D_MODEL), 0.02),
        "w_up": nrm((L, D_MODEL, 2 * D_FF), D_MODEL ** -0.5),
        "ffn_conv_w": nrm((L, FFN_CONV, D_FF), FFN_CONV ** -0.5),
        "ffn_conv_b": nrm((L, D_FF), 0.02),
        "w_down": nrm((L, D_FF, D_MODEL), D_FF ** -0.5),
        "norm_final": 1.0 + nrm((D_MODEL,), 0.02),
    }


def reference(x_prompt, x_sample, state_rwkv_shift, state_rwkv_wkv, state_mlstm_conv, state_mlstm_C,
              state_mlstm_n, state_mlstm_m, state_ffn_conv, meta_tokens, norm_mix, w_in, a_mu, a_w0, a_w2,
              a_a0, a_a2, a_g2, a_k_k, a_k_a, a_r_k, a_ln_w, a_ln_b, b_conv_w, b_conv_b, b_i_bias, b_f_bias,
              b_hn_w, w_out, norm_ffn, w_up, ffn_conv_w, ffn_conv_b, w_down, norm_final):
    layer_weights = (norm_mix, w_in, a_mu, a_w0, a_w2, a_a0, a_a2, a_g2, a_k_k, a_k_a, a_r_k, a_ln_w, a_ln_b,
                     b_conv_w, b_conv_b, b_i_bias, b_f_bias, b_hn_w, w_out, norm_ffn, w_up, ffn_conv_w,
                     ffn_conv_b, w_down)
    meta = jnp.broadcast_to(meta_tokens[None].astype(x_prompt.dtype), (x_prompt.shape[0], N_META, D_MODEL))
    xp = jnp.concatenate([meta, x_prompt], axis=1)
    hp, (p_rwkv_shift, p_rwkv_wkv, p_mlstm_conv, p_mlstm_C, p_mlstm_n, p_mlstm_m, p_ffn_conv) = trunk(
        xp, None, layer_weights, N_META)
    y_prompt = rmsnorm(hp, norm_final)[:, N_META:]
    states = (state_rwkv_shift, state_rwkv_wkv, state_mlstm_conv, state_mlstm_C, state_mlstm_n,
              state_mlstm_m, state_ffn_conv)
    hs, (s_rwkv_shift, s_rwkv_wkv, s_mlstm_conv, s_mlstm_C, s_mlstm_n, s_mlstm_m, s_ffn_conv) = trunk(
        x_sample, states, layer_weights, x_sample.shape[1])
    y_sample = rmsnorm(hs, norm_final)
    return (y_prompt, y_sample,
            p_rwkv_shift, p_rwkv_wkv, p_mlstm_conv, p_mlstm_C, p_mlstm_n, p_mlstm_m, p_ffn_conv,
            s_rwkv_shift, s_rwkv_wkv, s_mlstm_conv, s_mlstm_C, s_mlstm_n, s_mlstm_m, s_ffn_conv)
```

```python
import numpy as np
from contextlib import ExitStack
import concourse.bass as bass
import concourse.mybir as mybir
from concourse.bass_utils import run_bass_kernel_spmd

F32 = mybir.dt.float32
BF16 = mybir.dt.bfloat16
ALU = mybir.AluOpType
AF = mybir.ActivationFunctionType
AX = mybir.AxisListType

EPOCH = 16000
STRICT_WAR = False
TRANSITIVE = True
EMBED_WAIT = True
EMBED_ENG = ("pe", "act", "dve")
SEQ_THREADS = False
FLAG_OLDFFN = False
POOL_ENG = "dve"
RW_STOP = 0
SEQ_HEADS = False
SKIP_RWKV = False
SKIP_MLSTM = False
NDSEM = 20

D = 1024
DEPTH = 2
A_COLS = 1792
B_COLS = 2056
D_IN = 3848
D_FF = 2816
NFT = 22
NEG = -1.0e30
EXPM05 = 0.6065306597126334


class Buf:
    __slots__ = ("name", "w", "r", "psum")

    def __init__(self, name, psum=False):
        self.name = name
        self.w = None
        self.r = []
        self.psum = psum


def bufs(name, n):
    return [Buf(f"{name}{i}") for i in range(n)]


class _Rec:
    def __init__(self):
        self.call = None

    def __getattr__(self, name):
        def f(*a, **k):
            self.call = (name, a, k)
            return None
        return f


class Prog:
    ENG = ("pe", "act", "dve", "pool", "sp")

    def __init__(self, nc, stack, nepoch=8):
        self.nc = nc
        self.streams = {e: [] for e in self.ENG}
        self.count = {e: 0 for e in self.ENG}
        self.sems = {e: [stack.enter_context(nc.semaphore(f"s_{e}{i}")) for i in range(nepoch)]
                     for e in self.ENG}
        self.dsems = {q: [stack.enter_context(nc.semaphore(f"d_{q}{i}")) for i in range(NDSEM)]
                      for q in ("sp", "pool", "act")}
        self.dval = {q: [0] * NDSEM for q in self.dsems}
        self.dnext = {q: 0 for q in self.dsems}
        self.seen = {e: {} for e in self.ENG}
        self.know = {}
        self.nepoch = nepoch

    def _need(self, eng, ev, waits):
        if ev is None:
            return
        if ev[0] == "e":
            key = ("e", ev[1])
            v = ev[2]
        else:
            key = ("d", ev[1], ev[2])
            v = ev[3]
        if self.seen[eng].get(key, 0) >= v:
            return
        self.seen[eng][key] = v
        waits[key] = max(waits.get(key, 0), v)
        if TRANSITIVE:
            kn = self.know.get(ev)
            if kn:
                sn = self.seen[eng]
                for k2, v2 in kn.items():
                    if sn.get(k2, 0) < v2:
                        sn[k2] = v2
                        if k2 in waits and waits[k2] <= v2 and k2 != key:
                            del waits[k2]

    def _resolve(self, waits):
        out = []
        for key, v in waits.items():
            if key[0] == "e":
                ep = (v - 1) // EPOCH
                assert ep < self.nepoch, "too many instructions for semaphore epochs"
                out.append((self.sems[key[1]][ep], (v - 1) % EPOCH + 1))
            else:
                out.append((self.dsems[key[1]][key[2]], v))
        return out

    def op(self, eng, fn, reads=(), writes=(), inc=True):
        rec = _Rec()
        fn(rec)
        nm_, a_, k_ = rec.call
        fn = (lambda e, nm_=nm_, a_=a_, k_=k_: getattr(e, nm_)(*a_, **k_))
        waits = {}
        for b in reads:
            ev = b.w
            if ev is not None and not (eng == "pe" and ev[0] == "e" and ev[1] == "pe"):
                self._need(eng, ev, waits)
            if b.psum:
                for ev in b.r:
                    if ev[0] == "e" and ev[1] == eng:
                        continue
                    self._need(eng, ev, waits)
        for b in writes:
            ev = b.w
            if ev is not None and not (eng == "pe" and ev[0] == "e" and ev[1] == "pe"):
                self._need(eng, ev, waits)
            for ev in b.r:
                if ev[0] == "e" and ev[1] == eng and (eng == "pe" or not STRICT_WAR):
                    continue
                self._need(eng, ev, waits)
        gc = self.count[eng] + 1
        if inc:
            self.count[eng] = gc
        myev = ("e", eng, gc)
        if TRANSITIVE:
            kn = dict(self.seen[eng])
            kn[("e", eng)] = max(kn.get(("e", eng), 0), gc)
            self.know[myev] = kn
        for b in writes:
            b.w = myev
            b.r = []
        for b in reads:
            if b not in writes:
                b.r = [e for e in b.r if not (e[0] == "e" and e[1] == eng)] + [myev]
        ep = (gc - 1) // EPOCH
        assert ep < self.nepoch
        self.streams[eng].append((self._resolve(waits), fn, (self.sems[eng][ep], 1) if inc else None))

    def dma(self, q, out, in_, reads=(), writes=(), **kw):
        waits = {}
        for b in reads:
            self._need(q, b.w, waits)
        for b in writes:
            self._need(q, b.w, waits)
            for ev in b.r:
                self._need(q, ev, waits)
        idx = self.dnext[q]
        self.dnext[q] = (idx + 1) % NDSEM
        prev = self.dval[q][idx]
        if prev > 0:
            self._need(q, ("d", q, idx, prev), waits)
        val = prev + 16
        self.dval[q][idx] = val
        myev = ("d", q, idx, val)
        if TRANSITIVE:
            self.know[myev] = dict(self.seen[q])
        for b in writes:
            b.w = myev
            b.r = []
        for b in reads:
            b.r = b.r + [myev]
        fn = (lambda e, out=out, in_=in_, kw=kw: e.dma_start(out=out, in_=in_, **kw))
        self.streams[q].append((self._resolve(waits), fn, (self.dsems[q][idx], 16)))

    def finish(self, eng):
        waits = {}
        for q in self.dsems:
            for idx in range(NDSEM):
                if self.dval[q][idx] > 0:
                    self._need(eng, ("d", q, idx, self.dval[q][idx]), waits)
        self.streams[eng].append((self._resolve(waits), None, None))

    def emit(self):
        nc = self.nc
        with nc.Block() as block:
            def run(engh, name):
                for waits, fn, inc in self.streams[name]:
                    emb = None
                    if EMBED_WAIT and fn is not None and waits and name in EMBED_ENG:
                        emb = waits[-1]
                        waits = waits[:-1]
                    for sem, v in waits:
                        engh.wait_ge(sem, v)
                    if fn is not None:
                        ins = fn(engh)
                        if emb is not None:
                            ins._wait_ge(emb[0], emb[1])
                        if inc is not None:
                            ins.then_inc(inc[0], inc[1])

            @block.sync
            def _(e):
                run(e, "sp")

            @block.tensor
            def _(e):
                run(e, "pe")

            @block.scalar
            def _(e):
                run(e, "act")

            @block.vector
            def _(e):
                run(e, "dve")

            @block.gpsimd
            def _(e):
                run(e, "pool")


C_ID, C_TRII, C_TRIS, C_ONES, C_MNEG, C_MTNEG, C_LOS, C_BLK, C_HSEL, C_EL128, C_EL16, C_TRIIW, C_TRISW, C_ID2, C_END = (
    0, 128, 256, 384, 512, 640, 768, 896, 1024, 1026, 1154, 1282, 1410, 1538, 1602)


def make_consts():
    c = np.zeros((128, C_END), np.float32)
    i = np.arange(128)
    s = i[:, None]
    t = i[None, :]
    c[:, C_ID:C_ID + 128] = (s == t)
    c[:, C_TRII:C_TRII + 128] = (s <= t)
    c[:, C_TRIS:C_TRIS + 128] = (s < t)
    c[:, C_ONES:C_ONES + 128] = 1.0
    c[:, C_MNEG:C_MNEG + 128] = np.where(t <= s, 0.0, NEG)
    c[:, C_MTNEG:C_MTNEG + 128] = np.where(s <= t, 0.0, NEG)
    c[:, C_LOS:C_LOS + 128] = (s > t)
    c[:, C_BLK:C_BLK + 128] = ((s // 64) == (t // 64))
    c[:, C_HSEL] = (i < 64)
    c[:, C_HSEL + 1] = (i >= 64)
    c[127, C_EL128:C_EL128 + 128] = 1.0
    c[15, C_EL16:C_EL16 + 128] = 1.0
    c[:, C_TRIIW:C_TRIIW + 128] = -EXPM05 * (s <= t)
    c[:, C_TRISW:C_TRISW + 128] = -EXPM05 * (s < t)
    c[:, C_ID2:C_ID2 + 64] = ((s % 64) == i[None, :64])
    return c


LAST = {}
DBG = {}


def build_program(NST, NCH=2):
    NTOK = NST * NCH * 128
    NW = NCH * 128
    WP = NW + 8
    nc = bass.Bass("TRN2", target_bir_lowering=False)
    dram = lambda n, s, k, dt=F32: nc.dram_tensor(n, list(s), dt, kind=k).ap()
    I = "ExternalInput"
    O = "ExternalOutput"
    xp = dram("xp", [NTOK, D], I)
    meta = dram("meta", [16, D], I)
    xs = dram("xs", [2, 16, D], I)
    st_shift = dram("st_shift", [2, 2, A_COLS], I)
    st_wkv = dram("st_wkv", [2, 2, 8, 64, 64], I)
    st_conv = dram("st_conv", [2, 2, 3, 1024], I)
    st_C = dram("st_C", [2, 2, 4, 128, 128], I)
    st_n = dram("st_n", [2, 2, 4, 128], I)
    st_m = dram("st_m", [2, 2, 4], I)
    st_fconv = dram("st_fconv", [2, 2, 2, D_FF], I)
    consts = dram("consts", [128, C_END], I)
    W = {}
    for n, s in [("norm_mix", [2, D]), ("w_in", [2, D, D_IN]), ("a_mu", [2, A_COLS]), ("a_w0", [2, 512]),
                 ("a_w2", [2, 64, 512]), ("a_a0", [2, 512]), ("a_a2", [2, 64, 512]), ("a_g2", [2, 128, 512]),
                 ("a_k_k", [2, 512]), ("a_k_a", [2, 512]), ("a_r_k", [2, 512]), ("a_ln_w", [2, 512]),
                 ("a_ln_b", [2, 512]), ("b_conv_w", [2, 4, 1024]), ("b_conv_b", [2, 1024]), ("b_i_bias", [2, 4]),
                 ("b_f_bias", [2, 4]), ("b_hn_w", [2, 512]), ("w_out", [2, D, D]), ("norm_ffn", [2, D]),
                 ("w_up", [2, D, 2 * D_FF]), ("ffn_conv_w", [2, 3, D_FF]), ("ffn_conv_b", [2, D_FF]),
                 ("w_down", [2, D_FF, D]), ("norm_final", [D])]:
        W[n] = dram(n, s, I)
    y_p = dram("y_p", [NTOK, D], O)
    y_s = dram("y_s", [2, 16, D], O)
    o_shift = dram("o_shift", [2, 3, A_COLS], O)
    o_wkv = dram("o_wkv", [2, 3, 8, 64, 64], O)
    o_conv = dram("o_conv", [2, 3, 3, 1024], O)
    o_C = dram("o_C", [2, 3, 4, 128, 128], O)
    o_n = dram("o_n", [2, 3, 4, 128], O)
    o_m = dram("o_m", [2, 3, 4], O)
    o_fconv = dram("o_fconv", [2, 3, 2, D_FF], O)
    wb_in = dram("wb_in", [2, D, D_IN], "Internal", BF16)
    wb_out = dram("wb_out", [2, D, D], "Internal", BF16)
    wb_up = dram("wb_up", [2, D, 2 * D_FF], "Internal", BF16)
    wb_down = dram("wb_down", [2, D_FF, D], "Internal", BF16)
    b_scr = {}

    with ExitStack() as st:
        P = Prog(nc, st)
        def sb(n, s, dt=F32):
            nb = int(np.prod(s[1:])) * (2 if dt == BF16 else 4)
            LAST.setdefault("sbuf", []).append((n, nb))
            return st.enter_context(nc.sbuf_tensor(n, list(s), dt))
        pst = lambda n, s, dt=F32: st.enter_context(nc.psum_tensor(n, list(s), dt))

        cur = {"sti": -1}

        def dump(name, l, ci, ap, dt, rbufs):
            key = DBG.get(name)
            if key is None or key != (l, cur["sti"], ci):
                return
            o = dram("dbg_" + name, list(ap.shape), O, dt)
            P.dma("pool", o, ap, reads=rbufs)

        CT = sb("CT", [128, C_END]); bCT = Buf("CT")
        IDB = sb("IDB", [128, 128], BF16); bIDB = Buf("IDB")
        ONB = sb("ONB", [128, 128], BF16)
        P.dma("pool", CT[:], consts, writes=[bCT])
        P.op("dve", lambda e: e.tensor_copy(IDB[:], CT[:, C_ID:C_ID + 128]), reads=[bCT], writes=[bIDB])
        cs = lambda off, L, n=None: CT[0:L, off:off + (L if n is None else n)]

        PS = [pst(f"PS{i}", [128, 512]) for i in range(8)]
        bPS = [Buf(f"PS{i}", psum=True) for i in range(8)]
        PSB = [p[:].bitcast(BF16) for p in PS]

        _ev = [0]

        def evac(out, in_, r, w, eng=None):
            _ev[0] ^= 1
            if (eng == "act") or (eng is None and _ev[0]):
                P.op("act", lambda e: e.copy(out, in_), reads=r, writes=w)
            else:
                P.op("dve", lambda e: e.tensor_copy(out, in_), reads=r, writes=w)

        def mm(out, lhsT, rhs, r, w, start=True, stop=True, inc=True):
            P.op("pe", lambda e: e.matmul(out, lhsT, rhs, start=start, stop=stop), reads=r, writes=w, inc=inc)

        WSN = 4
        WSL = [sb(f"WS{i}", [128, 4096], BF16) for i in range(WSN)]; bWS = bufs("WS", WSN)
        STG = [WSL[i][:].bitcast(F32) for i in range(2)]
        bSTG = [bWS[0], bWS[1]]
        STB = [WSL[2][:, 0:2048], WSL[2][:, 2048:4096]]
        bSTB = bufs("STB", 2)
        k = 0
        for l in range(2):
            for (src, dst, rows, cols) in [(W["w_in"], wb_in, D, D_IN), (W["w_out"], wb_out, D, D),
                                           (W["w_up"], wb_up, D, 2 * D_FF), (W["w_down"], wb_down, D_FF, D)]:
                bsc_ = b_scr.setdefault((dst.tensor.name, l), Buf("scr"))
                for r0 in range(0, rows, 128):
                    for c0 in range(0, cols, 2048):
                        cw = min(2048, cols - c0)
                        i = k % 2
                        k += 1
                        P.dma("sp", STG[i][:, 0:cw], src[l, r0:r0 + 128, c0:c0 + cw], writes=[bSTG[i]])
                        h = cw // 2
                        P.op("act", lambda e, i=i, h=h: e.copy(STB[i][:, 0:h], STG[i][:, 0:h]),
                             reads=[bSTG[i]], writes=[bSTB[i]])
                        P.op("dve", lambda e, i=i, h=h, cw=cw: e.tensor_copy(STB[i][:, h:cw], STG[i][:, h:cw]),
                             reads=[bSTG[i]], writes=[bSTB[i]])
                        P.dma("pool", dst[l, r0:r0 + 128, c0:c0 + cw], STB[i][:, 0:cw], reads=[bSTB[i]], writes=[bsc_])

        def colload(name, src2d_T, ncol):
            t = sb(name, [128, ncol]); b = Buf(name)
            return t, b
        PRM = []
        bPRM = Buf("PRM")
        LWs = [sb(f"LW{i}", [128, 512]) for i in range(4)]
        bLW = Buf("LW")
        for l in range(2):
            d = {}
            d["mu"] = sb(f"mu{l}", [128, 14])
            d["kk"] = sb(f"kk{l}", [128, 4]); d["ka"] = sb(f"ka{l}", [128, 4]); d["rk"] = sb(f"rk{l}", [128, 4])
            d["a0"] = sb(f"a0{l}", [128, 4])
            d["cw"] = sb(f"cw{l}", [128, 4, 8]); d["cb"] = sb(f"cb{l}", [128, 8])
            d["fw"] = sb(f"fw{l}", [128, 3, NFT]); d["fb"] = sb(f"fb{l}", [128, NFT])
            d["g1"] = sb(f"g1{l}", [128, 8]); d["g2"] = sb(f"g2{l}", [128, 8])
            d["w2a2"] = sb(f"w2a2{l}", [128, 512], BF16); d["g2w"] = sb(f"g2w{l}", [128, 512], BF16)
            d["lw"] = LWs[0]; d["lb"] = LWs[1]; d["hw"] = LWs[2]; d["w0"] = LWs[3]
            d["ifb"] = sb(f"ifb{l}", [128, 8])
            PRM.append(d)
            ld = lambda dst, src: P.dma("pool", dst, src, writes=[bPRM], allow_slow_non_contiguous=True)
            ld(d["mu"][:], W["a_mu"][l].rearrange("(j p) -> p j", p=128))
            ld(d["kk"][:], W["a_k_k"][l].rearrange("(j p) -> p j", p=128))
            ld(d["ka"][:], W["a_k_a"][l].rearrange("(j p) -> p j", p=128))
            ld(d["rk"][:], W["a_r_k"][l].rearrange("(j p) -> p j", p=128))
            ld(d["a0"][:], W["a_a0"][l].rearrange("(j p) -> p j", p=128))
            for j in range(4):
                ld(d["cw"][:, j, :], W["b_conv_w"][l, j].rearrange("(j p) -> p j", p=128))
            ld(d["cb"][:], W["b_conv_b"][l].rearrange("(j p) -> p j", p=128))
            for j in range(3):
                ld(d["fw"][:, j, :], W["ffn_conv_w"][l, j].rearrange("(j p) -> p j", p=128))
            ld(d["fb"][:], W["ffn_conv_b"][l].rearrange("(j p) -> p j", p=128))
            ld(d["g1"][:], W["norm_mix"][l].rearrange("(j p) -> p j", p=128))
            ld(d["g2"][:], W["norm_ffn"][l].rearrange("(j p) -> p j", p=128))
            ld(d["ifb"][:, 0:4], W["b_i_bias"][l].partition_broadcast(128))
            ld(d["ifb"][:, 4:8], W["b_f_bias"][l].partition_broadcast(128))
            P.dma("pool", STG[0][0:64, 0:512], W["a_w2"][l], writes=[bSTG[0]])
            P.dma("pool", STG[0][64:128, 0:512], W["a_a2"][l], writes=[bSTG[0]])
            P.dma("pool", STG[0][:, 512:1024], W["a_g2"][l], writes=[bSTG[0]])
            P.op("dve", lambda e, d=d: e.tensor_copy(d["w2a2"][:], STG[0][:, 0:512]), reads=[bSTG[0]], writes=[bPRM])
            P.op("dve", lambda e, d=d: e.tensor_copy(d["g2w"][:], STG[0][:, 512:1024]), reads=[bSTG[0]], writes=[bPRM])
        GF = sb("GF", [128, 1024])
        P.dma("pool", GF[:], W["norm_final"].partition_broadcast(128), writes=[bPRM])

        NX = max(NCH, 3)
        X = sb("X", [128, NX, D]); bX = bufs("X", NX)
        HB = sb("HB", [128, D], BF16); bHB = Buf("HB")
        JK = sb("JK", [128, D]); bJK = Buf("JK")
        SM = sb("SM", [128, 16]); bSM = Buf("SM")
        HT = sb("HT", [128, 8, NW], BF16); bHT = Buf("HT")
        PT = sb("PT", [128, 14, WP]); bPT = bufs("PT", 14)
        assert 14 * WP * 4 >= NFT * NW * 2
        ZT = PT[:].rearrange("p j w -> p (j w)")[:, 0:NFT * NW // 2].bitcast(BF16).rearrange("p (f n) -> p f n", f=NFT)
        bZT = bufs("ZT", NFT)
        NKK = sb("NKK", [128, 4, NW], BF16); BBT = sb("BBT", [128, 4, NW], BF16); bNB = Buf("NB")
        TMPA = sb("TMPA", [128, NW]); bTA = Buf("TMPA")
        TMPB = sb("TMPB", [128, NW]); bTB = Buf("TMPB")
        TW = sb("TW", [128, NW], BF16); bTW = Buf("TW")
        SG = sb("SG", [128, NW], BF16); bSG = Buf("SG")
        QKT = sb("QKT", [128, 8, NW], BF16); bQK = Buf("QKT")
        S = [sb(f"S{i}", [128, 512]) for i in range(4)]; bS = bufs("S", 4)
        LDT = sb("LDT", [128, 512]); bLDT = Buf("LDT")
        AR = sb("AR", [128, 4, 2, 128], BF16); bAR = Buf("AR")
        BTt = sb("BTt", [128, 4, 128], BF16); KTt = sb("KTt", [128, 4, 128], BF16); bBK = Buf("BK")
        BHT = sb("BHT", [128, 4, 128], BF16); KHT = sb("KHT", [128, 4, 128], BF16); VBT = sb("VBT", [128, 4, 128], BF16)
        bBKH = Buf("BKH")
        TOK = sb("TOK", [128, 4, 512], BF16); bTOK = Buf("TOK")
        Y0 = sb("Y0", [128, 512]); bY0 = Buf("Y0")
        QT = sb("QT", [64, 8, 128], BF16); bQT = Buf("QT")
        MT = sb("MT", [64, 8, 64]); bMT = Buf("MT")
        ZS = sb("ZS", [64, 512]); bZS = Buf("ZS")
        DG = sb("DG", [128, 4, 64]); bDG = Buf("DG")
        Hs = [[sb(f"H{l}_{s}", [64, 512]) for s in range(3)] for l in range(2)]
        bH = [[Buf(f"H{l}_{s}") for s in range(3)] for l in range(2)]
        HBF = sb("HBF", [64, 512], BF16); bHBF = Buf("HBF")
        bYA = Buf("YA")
        YBb = sb("YBb", [128, 1024], BF16); bYB = Buf("YBb")
        YT = sb("YT", [128, 8, NW], BF16); bYT = Buf("YT")
        VO = sb("VO", [128, 512]); bVO = Buf("VO")
        VE = sb("VE", [128, 4, 129], BF16); bVE = Buf("VE")
        STb = sb("STb", [128, 512], BF16); bSTb = Buf("STb")
        KTOK = sb("KTOK", [128, 512], BF16); bKTOK = Buf("KTOK")
        KW = sb("KW", [128, 512], BF16); bKW = Buf("KW")
        CE = [[sb(f"CE{l}_{s}", [128, 4, 129]) for s in range(3)] for l in range(2)]
        bCE = [[Buf(f"CE{l}_{s}") for s in range(3)] for l in range(2)]
        CB = sb("CB", [128, 4, 129], BF16); bCB = Buf("CB")
        M0 = [[sb(f"M0{l}_{s}", [128, 4]) for s in range(3)] for l in range(2)]
        bM0 = [[Buf(f"M0{l}_{s}") for s in range(3)] for l in range(2)]
        ND = sb("ND", [128, 4, 129]); bND = Buf("ND")
        G8 = sb("G8", [128, 64]); bG8 = Buf("G8")
        U2 = [sb(f"U{i}", [128, WP]) for i in range(3)]; bU2 = bufs("U", 3)
        UA2 = [sb(f"UA{i}", [128, NW]) for i in range(3)]; bUA2 = bufs("UA", 3)
        CSH = [[sb(f"CSH{l}_{s}", [128, 14]) for s in range(3)] for l in range(2)]
        CCV = [[sb(f"CCV{l}_{s}", [128, 8, 3]) for s in range(3)] for l in range(2)]
        CFF = [[sb(f"CFF{l}_{s}", [128, NFT, 2]) for s in range(3)] for l in range(2)]
        bCAR = [[Buf(f"CAR{l}_{s}") for s in range(3)] for l in range(2)]
        ROW = sb("ROW", [4, 1536]); bROW = Buf("ROW")
        WG = sb("WG", [128, 64], BF16); bWG = Buf("WG")
        wsi = [0]

        def wslot():
            i = wsi[0] % WSN
            wsi[0] += 1
            return WSL[i], bWS[i]

        for l in range(2):
            P.op("pool", lambda e, l=l: e.memset(Hs[l][0][:], 0.0), writes=[bH[l][0]])
            P.op("pool", lambda e, l=l: e.memset(CE[l][0][:], 0.0), writes=[bCE[l][0]])
            P.op("pool", lambda e, l=l: e.memset(M0[l][0][:], 0.0), writes=[bM0[l][0]])
            P.op("pool", lambda e, l=l: e.memset(CSH[l][0][:], 0.0), writes=[bCAR[l][0]])
            P.op("pool", lambda e, l=l: e.memset(CCV[l][0][:], 0.0), writes=[bCAR[l][0]])
            P.op("pool", lambda e, l=l: e.memset(CFF[l][0][:], 0.0), writes=[bCAR[l][0]])
            for s in (1, 2):
                b = s - 1
                P.dma("pool", STG[1][0:64, 0:512].rearrange("p (h k) -> p h k", h=8),
                      st_wkv[l, b].rearrange("h v k -> v h k"), writes=[bSTG[1]])
                for h in range(8):
                    mm(PS[0][0:64, h * 64:(h + 1) * 64], STG[1][0:64, h * 64:(h + 1) * 64], CT[0:64, C_ID:C_ID + 64],
                       r=[bSTG[1], bCT], w=[bPS[0]])
                evac(Hs[l][s][:], PS[0][0:64, :], r=[bPS[0]], w=[bH[l][s]])
                P.dma("pool", CE[l][s][:, :, 0:128], st_C[l, b].rearrange("h d e -> d h e"), writes=[bCE[l][s]])
                P.dma("pool", CE[l][s][:, :, 128:129], st_n[l, b].rearrange("h (d o) -> d h o", o=1),
                      writes=[bCE[l][s]], allow_slow_non_contiguous=True)
                P.dma("pool", M0[l][s][:], st_m[l, b].partition_broadcast(128), writes=[bM0[l][s]])
                for (c0_, nt_) in ((0, 12), (1536, 2)):
                    P.dma("pool", ROW[0:1, 0:nt_ * 128], st_shift[l, b:b + 1, c0_:c0_ + nt_ * 128], writes=[bROW])
                    for j in range(nt_):
                        jj_ = c0_ // 128 + j
                        mm(PS[1][:, jj_:jj_ + 1], ROW[0:1, j * 128:(j + 1) * 128], CT[0:1, C_ID:C_ID + 1], r=[bROW, bCT], w=[bPS[1]])
                evac(CSH[l][s][:], PS[1][:, 0:14], r=[bPS[1]], w=[bCAR[l][s]])
                P.dma("pool", ROW[0:3, 0:1024], st_conv[l, b], writes=[bROW])
                for j in range(8):
                    mm(PS[1][:, 32 + 3 * j:35 + 3 * j], ROW[0:3, j * 128:(j + 1) * 128], CT[0:3, C_ID:C_ID + 3],
                       r=[bROW, bCT], w=[bPS[1]])
                evac(CCV[l][s][:], PS[1][:, 32:56].rearrange("p (j k) -> p j k", k=3), r=[bPS[1]], w=[bCAR[l][s]])
                for c0_ in (0, 1408):
                    P.dma("pool", ROW[0:2, 0:1408], st_fconv[l, b, :, c0_:c0_ + 1408], writes=[bROW])
                    for j in range(11):
                        jj_ = c0_ // 128 + j
                        mm(PS[1][:, 64 + 2 * jj_:66 + 2 * jj_], ROW[0:2, j * 128:(j + 1) * 128], CT[0:2, C_ID:C_ID + 2],
                           r=[bROW, bCT], w=[bPS[1]])
                evac(CFF[l][s][:], PS[1][:, 64:64 + 2 * NFT].rearrange("p (j k) -> p j k", k=2), r=[bPS[1]], w=[bCAR[l][s]])

        def rstd_of(xt, L, col):
            P.op("act", lambda e: e.activation(JK[0:L, :], xt, AF.Square, accum_out=SM[0:L, col:col + 1]),
                 reads=[bXr], writes=[bJK, bSM])
            P.op("act", lambda e: e.activation(SM[0:L, col:col + 1], SM[0:L, col:col + 1], AF.Ln, bias=1e-6, scale=1.0 / D),
                 reads=[bSM], writes=[bSM])
            P.op("act", lambda e: e.activation(SM[0:L, col:col + 1], SM[0:L, col:col + 1], AF.Exp, scale=-0.5),
                 reads=[bSM], writes=[bSM])

        HBx = YBb; bHBx = bYA
        SMN = sb("SMN", [128, 4]); bSMN = bufs("SMN", 4)

        def norm_gen(ci, off, L, gcol, k):
            hb, bhb = (HB, bHB) if k % 2 == 0 else (HBx, bHBx)
            pbk = 2 + (k % 2)
            P.op("act", lambda e: e.activation(JK[0:L, :], X[0:L, ci, :], AF.Square, accum_out=SMN[0:L, ci:ci + 1]),
                 reads=[bX[ci]], writes=[bJK, bSMN[ci]])
            yield
            P.op("act", lambda e: e.activation(SMN[0:L, ci:ci + 1], SMN[0:L, ci:ci + 1], AF.Ln, bias=1e-6, scale=1.0 / D),
                 reads=[bSMN[ci]], writes=[bSMN[ci]])
            yield
            P.op("act", lambda e: e.activation(SMN[0:L, ci:ci + 1], SMN[0:L, ci:ci + 1], AF.Exp, scale=-0.5),
                 reads=[bSMN[ci]], writes=[bSMN[ci]])
            yield
            P.op("dve", lambda e: e.tensor_scalar(hb[0:L, :], X[0:L, ci, :], SMN[0:L, ci:ci + 1], None, ALU.mult),
                 reads=[bX[ci], bSMN[ci]], writes=[bhb])
            yield
            for kc in range(8):
                P.op("pe", lambda e, kc=kc: e.transpose(PSB[pbk][:, kc * 128:kc * 128 + L], hb[0:L, kc * 128:(kc + 1) * 128], IDB[0:L, 0:L]),
                     reads=[bhb, bIDB], writes=[bPS[pbk]], inc=(kc == 7))
            yield
            P.op("dve", lambda e: e.tensor_tensor(
                HT[:, :, off:off + L], PSB[pbk].rearrange("p (k t) -> p k t", k=8)[:, :, 0:L],
                gcol.unsqueeze(2).to_broadcast([128, 8, L]), ALU.mult),
                reads=[bPS[pbk], bPRM], writes=[bHT])
            yield

        def norm_T(chunks, gcol):
            run_pipeline([norm_gen(ci, off, L, gcol, k) for k, (ci, off, L, seq) in enumerate(chunks)], 2)

        bXr = None

        def load_w_cols(wb, l, c0, ncols):
            ws, bw = wslot()
            v = ws[:, 0:8 * ncols].rearrange("p (k c) -> p k c", k=8)
            P.dma("sp", v, wb[l, :, c0:c0 + ncols].rearrange("(k p) c -> p k c", p=128), reads=[b_scr[(wb.tensor.name, l)]], writes=[bw])
            return v, bw

        def process_st(sti):
            nonlocal bXr
            mini = (sti == 0)
            if mini:
                chunks = [(0, 0, 16, 0), (1, 16, 16, 1), (2, 32, 16, 2)]
                N = 48
            else:
                t0 = (sti - 1) * NW
                chunks = [(c, c * 128, 128, 0) for c in range(NCH)]
                N = NW
            segs = []
            if mini:
                for (ci, off, L, seq) in chunks:
                    segs.append((seq, off, L, off + 3 * (ci + 1)))
            else:
                segs.append((0, 0, N, 3))
            pcol = {}
            for (seq, off, L, po) in segs:
                for (ci, coff, cL, cseq) in chunks:
                    if off <= coff < off + L:
                        pcol[ci] = po + (coff - off)
            last_of_seq = {}
            for (seq, off, L, po) in segs:
                last_of_seq[seq] = (seq > 0) or (sti == NST)
            for (ci, off, L, seq) in chunks:
                if mini:
                    src = meta if seq == 0 else xs[seq - 1]
                else:
                    src = xp[t0 + off:t0 + off + L, :]
                P.dma("pool", X[0:L, ci, :], src, writes=[bX[ci]])

            for l in range(2):
                pr = PRM[l]
                for i_, nm_ in enumerate(["a_ln_w", "a_ln_b", "b_hn_w", "a_w0"]):
                    P.dma("pool", LWs[i_][:], W[nm_][l].partition_broadcast(128), writes=[bLW])
                norm_T(chunks, pr["g1"][:])
                def gen_rwkv_in():
                    order = [12, 13, 4, 5, 6, 7, 0, 1, 2, 3, 8, 9, 10, 11]
                    wvs = {}
                    for j in order:
                        g, jj = j // 4, j % 4
                        if g not in wvs:
                            wvs[g] = load_w_cols(wb_in, l, g * 512, (4 if g < 3 else 2) * 128)
                        wv, bw = wvs[g]
                        pb = 3 + (j % 2)
                        for kc in range(8):
                            mm(PS[pb][:, 0:N], wv[:, kc, jj * 128:(jj + 1) * 128], HT[:, kc, 0:N], r=[bw, bHT], w=[bPS[pb]],
                               start=(kc == 0), stop=(kc == 7), inc=(kc == 7))
                        yield
                        for (seq, off, L, po) in segs:
                            evac(PT[:, j, po:po + L], PS[pb][:, off:off + L], r=[bPS[pb]], w=[bPT[j]] + bZT, eng="act")
                        yield
                        for (seq, off, L, po) in segs:
                            P.op("dve", lambda e, seq=seq, po=po, j=j: e.tensor_copy(PT[:, j, po - 1:po], CSH[l][seq][:, j:j + 1]),
                                 reads=[bCAR[l][seq]], writes=[bPT[j]])
                            P.op("dve", lambda e, seq=seq, po=po, L=L, j=j: e.tensor_copy(CSH[l][seq][:, j:j + 1], PT[:, j, po + L - 1:po + L]),
                                 reads=[bPT[j]], writes=[bCAR[l][seq]])
                            P.op("dve", lambda e, j=j, po=po, L=L: e.tensor_tensor(TMPA[:, 0:L], PT[:, j, po - 1:po + L - 1], PT[:, j, po:po + L], ALU.subtract),
                                 reads=[bPT[j]], writes=[bTA])
                            P.op("dve", lambda e, j=j, po=po, L=L: e.scalar_tensor_tensor(PT[:, j, po:po + L], TMPA[:, 0:L], pr["mu"][:, j:j + 1],
                                                                                         PT[:, j, po:po + L], ALU.mult, ALU.add),
                                 reads=[bTA, bPT[j], bPRM], writes=[bPT[j]])
                        yield
                    for (seq, off, L, po) in segs:
                        sl = slice(po, po + L)
                        ol = slice(off, off + L)
                        P.op("act", lambda e, sl=sl, ol=ol: e.activation(TW[0:64, ol], PT[0:64, 12, sl], AF.Tanh), reads=[bPT[12]], writes=[bTW])
                        P.op("dve", lambda e, sl=sl, ol=ol: e.tensor_copy(TW[64:128, ol], PT[64:128, 12, sl]), reads=[bPT[12]], writes=[bTW])
                        P.op("act", lambda e, sl=sl, ol=ol: e.activation(SG[:, ol], PT[:, 13, sl], AF.Sigmoid), reads=[bPT[13]], writes=[bSG])
                        yield
                        L2 = 2 * L
                        for p in range(2):
                            for jj in range(2):
                                j = 2 * p + jj
                                mm(PS[5 + p][:, jj * L:(jj + 1) * L], pr["w2a2"][64:128, j * 128:(j + 1) * 128], TW[64:128, ol], r=[bPRM, bTW], w=[bPS[5 + p]])
                                P.op("dve", lambda e, j=j, jj=jj, p=p: e.tensor_scalar(S[2 + p][:, jj * L:(jj + 1) * L], PT[:, 4 + j, sl], pr["kk"][:, j:j + 1], None, ALU.mult),
                                     reads=[bPT[4 + j], bPRM], writes=[bS[2 + p]])
                            yield
                            P.op("dve", lambda e, p=p: e.tensor_tensor(HB[:, p * L2:(p + 1) * L2], S[2 + p][:, 0:L2], S[2 + p][:, 0:L2], ALU.mult), reads=[bS[2 + p]], writes=[bHB])
                            for jj in range(2):
                                j = 2 * p + jj
                                P.op("act", lambda e, j=j, jj=jj, p=p: e.activation(S[p][:, jj * L:(jj + 1) * L], PS[5 + p][:, jj * L:(jj + 1) * L], AF.Sigmoid, bias=pr["a0"][:, j:j + 1]),
                                     reads=[bPS[5 + p], bPRM], writes=[bS[p]])
                            yield
                            pbk = 2 if p == 0 else 7
                            P.op("pe", lambda e, p=p, pbk=pbk: e.matmul(PS[pbk][:, 0:L2], BLKB[:, :], HB[:, p * L2:(p + 1) * L2], start=True, stop=True),
                                 reads=[bBLKB, bHB], writes=[bPS[pbk]])
                            yield
                        for p in range(2):
                            pbk = 2 if p == 0 else 7
                            P.op("act", lambda e, p=p, pbk=pbk: e.activation(JK[:, p * L2:(p + 1) * L2], PS[pbk][:, 0:L2], AF.Ln, bias=1e-12), reads=[bPS[pbk]], writes=[bJK])
                        P.op("act", lambda e: e.activation(JK[:, 0:2 * L2], JK[:, 0:2 * L2], AF.Exp, scale=-0.5), reads=[bJK], writes=[bJK])
                        yield
                        for p in range(2):
                            v3 = lambda t: t[:, 0:L2].rearrange("p (a t) -> p a t", a=2)
                            P.op("dve", lambda e, p=p: e.tensor_tensor(S[2 + p][:, 0:L2], S[2 + p][:, 0:L2], JK[:, p * L2:(p + 1) * L2], ALU.mult), reads=[bS[2 + p], bJK], writes=[bS[2 + p]])
                            yield
                            P.op("dve", lambda e, p=p: e.tensor_scalar(NKK[:, 2 * p:2 * p + 2, ol], v3(S[2 + p]), -1.0, None, ALU.mult), reads=[bS[2 + p]], writes=[bNB])
                            P.op("dve", lambda e, p=p: e.tensor_tensor(BBT[:, 2 * p:2 * p + 2, ol], v3(S[2 + p]), v3(S[p]), ALU.mult), reads=[bS[2 + p], bS[p]], writes=[bNB])
                            for jj in range(2):
                                j = 2 * p + jj
                                P.op("dve", lambda e, j=j, jj=jj, p=p: e.tensor_scalar(S[p][:, jj * L:(jj + 1) * L], S[p][:, jj * L:(jj + 1) * L], -1.0, pr["ka"][:, j:j + 1], ALU.add, ALU.mult),
                                     reads=[bS[p], bPRM], writes=[bS[p]])
                            yield
                            P.op("dve", lambda e, p=p: e.scalar_tensor_tensor(PT[:, 4 + 2 * p:6 + 2 * p, sl], v3(S[p]), 1.0, PT[:, 4 + 2 * p:6 + 2 * p, sl], ALU.add, ALU.mult),
                                 reads=[bS[p], bPT[4 + 2 * p], bPT[5 + 2 * p]], writes=[bPT[4 + 2 * p], bPT[5 + 2 * p]])
                            yield

                def gen_qk_in():
                    for g in range(2):
                        wv, bw = load_w_cols(wb_in, l, A_COLS + g * 512, 512)
                        for jj in range(4):
                            j = g * 4 + jj
                            pb = j % 2
                            U, bU, UA, bUA = U2[j % 2], bU2[j % 2], UA2[j % 2], bUA2[j % 2]
                            for kc in range(8):
                                mm(PS[pb][:, 0:N], wv[:, kc, jj * 128:(jj + 1) * 128], HT[:, kc, 0:N], r=[bw, bHT], w=[bPS[pb]],
                                   start=(kc == 0), stop=(kc == 7), inc=(kc == 7))
                            yield
                            for (seq, off, L, po) in segs:
                                evac(U[:, po:po + L], PS[pb][:, off:off + L], r=[bPS[pb]], w=[bU], eng="act")
                            yield
                            for (seq, off, L, po) in segs:
                                P.op("dve", lambda e, j=j, seq=seq, po=po: e.tensor_copy(U[:, po - 3:po], CCV[l][seq][:, j, :]),
                                     reads=[bCAR[l][seq]], writes=[bU])
                                P.op("dve", lambda e, j=j, seq=seq, po=po, L=L: e.tensor_copy(CCV[l][seq][:, j, :], U[:, po + L - 3:po + L]),
                                     reads=[bU], writes=[bCAR[l][seq]])
                            yield
                            for (seq, off, L, po) in segs:
                                P.op("act", lambda e, j=j, po=po, L=L, off=off: e.activation(UA[:, off:off + L], U[:, po - 3:po - 3 + L], AF.Identity,
                                                                                              bias=pr["cb"][:, j:j + 1], scale=pr["cw"][:, 0, j:j + 1]),
                                     reads=[bU, bPRM], writes=[bUA])
                            yield
                            for tpi in (1, 2, 3):
                                for (seq, off, L, po) in segs:
                                    P.op(POOL_ENG, lambda e, j=j, po=po, L=L, off=off, tpi=tpi: e.scalar_tensor_tensor(
                                        UA[:, off:off + L], U[:, po - 3 + tpi:po - 3 + tpi + L], pr["cw"][:, tpi, j:j + 1], UA[:, off:off + L], ALU.mult, ALU.add),
                                        reads=[bU, bPRM, bUA], writes=[bUA])
                                yield
                            if j >= 4:
                                P.op("act", lambda e, j=j, N=N: e.activation(UA[:, 0:N], UA[:, 0:N], AF.Silu), reads=[bUA], writes=[bUA])
                                yield
                                P.op("dve", lambda e, j=j, N=N: e.tensor_scalar(QKT[:, j, 0:N], UA[:, 0:N], 128.0 ** -0.5, None, ALU.mult),
                                     reads=[bUA], writes=[bQK])
                            else:
                                P.op("act", lambda e, j=j, N=N: e.activation(QKT[:, j, 0:N], UA[:, 0:N], AF.Silu), reads=[bUA], writes=[bQK])
                            yield

                run_threads([gen_rwkv_in(), gen_qk_in()])
                for (seq, off, L, po) in segs:
                    if last_of_seq[seq]:
                        tl = off + L - 1
                        for g in range(4):
                            ncol = 512 if g < 3 else 256
                            wv, bw = load_w_cols(wb_in, l, g * 512, ncol)
                            for kc in range(8):
                                mm(PS[5][0:1, 0:ncol], HT[:, kc, tl:tl + 1], wv[:, kc, 0:ncol], r=[bw, bHT], w=[bPS[5]],
                                   start=(kc == 0), stop=(kc == 7), inc=(kc == 7))
                            evac(ROW[0:1, (g % 3) * 512:(g % 3) * 512 + ncol], PS[5][0:1, 0:ncol], r=[bPS[5]], w=[bROW])
                            P.dma("pool", o_shift[l, seq:seq + 1, g * 512:g * 512 + ncol], ROW[0:1, (g % 3) * 512:(g % 3) * 512 + ncol], reads=[bROW])
                for (seq, off, L, po) in segs:
                    if last_of_seq[seq]:
                        tl = off + L - 3
                        for g in range(2):
                            wv, bw = load_w_cols(wb_in, l, A_COLS + g * 512, 512)
                            for kc in range(8):
                                mm(PS[5][0:3, 0:512], HT[:, kc, tl:tl + 3], wv[:, kc, 0:512], r=[bw, bHT], w=[bPS[5]],
                                   start=(kc == 0), stop=(kc == 7), inc=(kc == 7))
                            evac(ROW[0:3, g * 512:(g + 1) * 512], PS[5][0:3, 0:512], r=[bPS[5]], w=[bROW])
                        P.dma("pool", o_conv[l, seq], ROW[0:3, 0:1024], reads=[bROW])

                wv_vo = []
                for g in range(2):
                    wv_vo.append(load_w_cols(wb_in, l, A_COLS + 1024 + g * 512, 512))
                bw_g = bWG
                wv_gt = WG[:, 0:64].rearrange("p (k c) -> p k c", k=8)
                P.dma("sp", wv_gt, wb_in[l, :, D_IN - 8:D_IN].rearrange("(k p) c -> p k c", p=128), reads=[b_scr[(wb_in.tensor.name, l)]], writes=[bw_g])

                nck_ = len(chunks)
                Rdone = [False] * nck_
                Mdone = [False] * nck_
                Yiss = [False] * nck_
                HSIG.clear()
                Rg = Mg = None
                rc = mc = 0
                rcur = mcur = -1
                while not all(Yiss):
                    if Rg is None and rc < nck_ and (rc == 0 or Yiss[rc - 1]):
                        (ci, off, L, seq) = chunks[rc]
                        Rg = rwkv_gen(l, pr, ci, off, L, seq, pcol[ci])
                        rcur = rc
                        rc += 1
                    if Mg is None and mc < nck_ and (mc == 0 or HSIG.get(chunks[mc - 1][0], False)):
                        (ci, off, L, seq) = chunks[mc]
                        Mg = mlstm_gen(l, pr, ci, off, L, seq, wv_vo, (wv_gt, bw_g), guard=(lambda k_=mc: k_ == 0 or Yiss[k_ - 1]))
                        mcur = mc
                        mc += 1
                    if Rg is not None:
                        mdone["m"] = Mdone[rcur] and (Mg is None)
                        try:
                            next(Rg)
                        except StopIteration:
                            Rdone[rcur] = True
                            Rg = None
                    for _rep in range(2):
                        if Mg is not None:
                            try:
                                next(Mg)
                            except StopIteration:
                                Mdone[mcur] = True
                                Mg = None
                    for k_ in range(nck_):
                        if Rdone[k_] and Mdone[k_] and not Yiss[k_]:
                            (ci, off, L, seq) = chunks[k_]
                            dump("y", l, ci, YBb[0:L, :], BF16, [bYA])
                            for kc in range(8):
                                P.op("pe", lambda e, kc=kc, L=L: e.transpose(PSB[2][:, kc * 128:kc * 128 + L], YBb[0:L, kc * 128:(kc + 1) * 128], IDB[0:L, 0:L]),
                                     reads=[bYA, bIDB], writes=[bPS[2]], inc=(kc == 7))
                            evac(YT[:, :, off:off + L], PSB[2].rearrange("p (k t) -> p k t", k=8)[:, :, 0:L], r=[bPS[2]], w=[bYT])
                            Yiss[k_] = True
                for (seq, off, L, po) in segs:
                    if last_of_seq[seq]:
                        for h in range(8):
                            mm(PS[0][0:64, h * 64:(h + 1) * 64], Hs[l][seq][:, h * 64:(h + 1) * 64], CT[0:64, C_ID:C_ID + 64],
                               r=[bH[l][seq], bCT], w=[bPS[0]])
                        evac(ZS[:, :], PS[0][0:64, :], r=[bPS[0]], w=[bZS])
                        P.dma("pool", o_wkv[l, seq].rearrange("h v k -> v h k"), ZS[:, :].rearrange("p (h k) -> p h k", h=8), reads=[bZS])
                        P.dma("pool", o_C[l, seq].rearrange("h d e -> d h e"), CE[l][seq][:, :, 0:128], reads=[bCE[l][seq]])
                        P.dma("pool", o_n[l, seq].rearrange("h (d o) -> d h o", o=1), CE[l][seq][:, :, 128:129], reads=[bCE[l][seq]],
                              allow_slow_non_contiguous=True)
                        P.dma("pool", o_m[l, seq:seq + 1, :], M0[l][seq][0:1, :], reads=[bM0[l][seq]])

                wo = []
                for g in range(2):
                    ws, bw = wslot()
                    v = ws[:, 0:4096].rearrange("p (k c) -> p k c", k=4)
                    P.dma("sp", v, wb_out[l, g * 512:(g + 1) * 512, :].rearrange("(k p) c -> p k c", p=128), reads=[b_scr[(wb_out.tensor.name, l)]], writes=[bw])
                    wo.append((v, bw))
                for (ci, off, L, seq) in chunks:
                    for hf in range(2):
                        pb = 3 + hf
                        for kc in range(8):
                            v, bw = wo[kc // 4]
                            mm(PS[pb][0:L, :], YT[:, kc, off:off + L], v[:, kc % 4, hf * 512:(hf + 1) * 512], r=[bYT, bw], w=[bPS[pb]],
                               start=(kc == 0), stop=(kc == 7), inc=(kc == 7))
                        P.op("dve", lambda e, ci=ci, L=L, hf=hf, pb=pb: e.tensor_tensor(X[0:L, ci, hf * 512:(hf + 1) * 512], X[0:L, ci, hf * 512:(hf + 1) * 512],
                                                                                        PS[pb][0:L, :], ALU.add),
                             reads=[bPS[pb], bX[ci]], writes=[bX[ci]])

                for (ci, off, L, seq) in chunks:
                    dump("xmix", l, ci, X[0:L, ci, :], F32, [bX[ci]])
                norm_T(chunks, pr["g2"][:])
                wup = {}

                def ffn_gen(f):
                    if f % 4 == 0:
                        ncw = min(4, NFT - f) * 128
                        wup[f // 4] = (load_w_cols(wb_up, l, f * 128, ncw), load_w_cols(wb_up, l, D_FF + f * 128, ncw))
                    (wa, bwa), (wg, bwg) = wup[f // 4]
                    fo = (f % 4) * 128
                    U, bU, UA, bUA = U2[f % 3], bU2[f % 3], UA2[f % 3], bUA2[f % 3]
                    pa_i, pg_i = (f % 3), 3 + (f % 3)
                    for kc in range(8):
                        mm(PS[pa_i][:, 0:N], wa[:, kc, fo:fo + 128], HT[:, kc, 0:N], r=[bwa, bHT], w=[bPS[pa_i]], start=(kc == 0), stop=(kc == 7), inc=(kc == 7))
                    for kc in range(8):
                        mm(PS[pg_i][:, 0:N], wg[:, kc, fo:fo + 128], HT[:, kc, 0:N], r=[bwg, bHT], w=[bPS[pg_i]], start=(kc == 0), stop=(kc == 7), inc=(kc == 7))
                    yield
                    for (seq, off, L, po) in segs:
                        P.op("act", lambda e, po=po, off=off, L=L: e.copy(U[:, po:po + L], PS[pa_i][:, off:off + L]), reads=[bPS[pa_i]], writes=[bU])
                    yield
                    for (seq, off, L, po) in segs:
                        P.op("dve", lambda e, f=f, seq=seq, po=po: e.tensor_copy(U[:, po - 2:po], CFF[l][seq][:, f, :]), reads=[bCAR[l][seq]], writes=[bU])
                        P.op("dve", lambda e, f=f, seq=seq, po=po, L=L: e.tensor_copy(CFF[l][seq][:, f, :], U[:, po + L - 2:po + L]), reads=[bU], writes=[bCAR[l][seq]])
                    yield
                    for (seq, off, L, po) in segs:
                        P.op("act", lambda e, f=f, po=po, L=L, off=off: e.activation(UA[:, off:off + L], U[:, po - 2:po - 2 + L], AF.Identity,
                                                                                      bias=pr["fb"][:, f:f + 1], scale=pr["fw"][:, 0, f:f + 1]), reads=[bU, bPRM], writes=[bUA])
                    yield
                    for tpi in (1, 2):
                        for (seq, off, L, po) in segs:
                            P.op(POOL_ENG, lambda e, f=f, po=po, L=L, off=off, tpi=tpi: e.scalar_tensor_tensor(
                                UA[:, off:off + L], U[:, po - 2 + tpi:po - 2 + tpi + L], pr["fw"][:, tpi, f:f + 1], UA[:, off:off + L], ALU.mult, ALU.add),
                                reads=[bU, bPRM, bUA], writes=[bUA])
                        yield
                    P.op("act", lambda e, N=N: e.activation(UA[:, 0:N], UA[:, 0:N], AF.Silu), reads=[bUA], writes=[bUA])
                    yield
                    P.op("dve", lambda e, f=f, N=N: e.tensor_tensor(ZT[:, f, 0:N], UA[:, 0:N], PS[pg_i][:, 0:N], ALU.mult), reads=[bUA, bPS[pg_i]], writes=[bZT[f]] + bPT)
                    yield

                run_pipeline([ffn_gen(f) for f in range(NFT)], 3)
                for (seq, off, L, po) in segs:
                    if last_of_seq[seq]:
                        tl = off + L - 2
                        for g in range(6):
                            ncol = 512 if g < 5 else 256
                            wv, bw = load_w_cols(wb_up, l, g * 512, ncol)
                            for kc in range(8):
                                mm(PS[5][0:2, 0:ncol], HT[:, kc, tl:tl + 2], wv[:, kc, 0:ncol], r=[bw, bHT], w=[bPS[5]],
                                   start=(kc == 0), stop=(kc == 7), inc=(kc == 7))
                            evac(ROW[0:2, (g % 3) * 512:(g % 3) * 512 + ncol], PS[5][0:2, 0:ncol], r=[bPS[5]], w=[bROW])
                            P.dma("pool", o_fconv[l, seq, :, g * 512:g * 512 + ncol], ROW[0:2, (g % 3) * 512:(g % 3) * 512 + ncol], reads=[bROW])
                nck = len(chunks)
                pbase = 8 - 2 * nck
                for g in range(6):
                    nf = 4 if g < 5 else 2
                    ws, bw = wslot()
                    v = ws[:, 0:nf * 1024].rearrange("p (k c) -> p k c", k=nf)
                    P.dma("sp", v, wb_down[l, g * 512:g * 512 + nf * 128, :].rearrange("(k p) c -> p k c", p=128),
                          reads=[b_scr[(wb_down.tensor.name, l)]], writes=[bw])
                    for (ci, off, L, seq) in chunks:
                        for hf in range(2):
                            pb = pbase + 2 * ci + hf
                            for ff in range(nf):
                                f = g * 4 + ff
                                mm(PS[pb][0:L, :], ZT[:, f, off:off + L], v[:, ff, hf * 512:(hf + 1) * 512], r=[bZT[f], bw], w=[bPS[pb]],
                                   start=(f == 0), stop=(f == NFT - 1), inc=(ff == nf - 1))
                for (ci, off, L, seq) in chunks:
                    for hf in range(2):
                        pb = pbase + 2 * ci + hf
                        P.op("dve", lambda e, ci=ci, L=L, hf=hf, pb=pb: e.tensor_tensor(X[0:L, ci, hf * 512:(hf + 1) * 512], X[0:L, ci, hf * 512:(hf + 1) * 512],
                                                                                        PS[pb][0:L, :], ALU.add),
                             reads=[bPS[pb], bX[ci]], writes=[bX[ci]])
            for (ci, off, L, seq) in chunks:
                if mini and seq == 0:
                    continue
                bXr = bX[ci]
                rstd_of(X[0:L, ci, :], L, 1)
                P.op("dve", lambda e, ci=ci, L=L: e.scalar_tensor_tensor(JK[0:L, :], X[0:L, ci, :], SM[0:L, 1:2], GF[0:L, :], ALU.mult, ALU.mult),
                     reads=[bX[ci], bSM, bPRM], writes=[bJK])
                dst = y_s[seq - 1] if mini else y_p[t0 + off:t0 + off + L, :]
                P.dma("pool", dst, JK[0:L, :], reads=[bJK])

        BLKB = sb("BLKB", [128, 128], BF16); bBLKB = Buf("BLKB")
        P.op("dve", lambda e: e.tensor_copy(BLKB[:], CT[:, C_BLK:C_BLK + 128]), reads=[bCT], writes=[bBLKB])
        M2 = sb("M2", [128, 2, 128]); bM2 = Buf("M2")
        P.op("dve", lambda e: e.tensor_copy(M2[:, 0, :], CT[:, C_TRIS:C_TRIS + 128]), reads=[bCT], writes=[bM2])
        P.op("dve", lambda e: e.tensor_copy(M2[:, 1, :], CT[:, C_TRII:C_TRII + 128]), reads=[bCT], writes=[bM2])

        def run_pipeline(gens, depth):
            gens = list(gens)
            live = []
            while gens or live:
                while gens and len(live) < depth:
                    live.append(gens.pop(0))
                nl = []
                for g_ in live:
                    try:
                        next(g_)
                        nl.append(g_)
                    except StopIteration:
                        pass
                live = nl

        def run_threads(gens):
            gens = list(gens)
            if SEQ_THREADS:
                for g_ in gens:
                    for _ in g_:
                        pass
                return
            while gens:
                nxt = []
                for g_ in gens:
                    try:
                        next(g_)
                        nxt.append(g_)
                    except StopIteration:
                        pass
                gens = nxt

        NSLOT = 4
        SABs = [sb(f"SABs{i}", [128, 2, 2, 128], BF16) for i in range(NSLOT)]; bSABs = bufs("SABs", NSLOT)
        SN2s = [[sb(f"SN2s{i}_{k}", [128, 2, 128], BF16) for k in range(2)] for i in range(NSLOT)]
        bSN2s = [bufs(f"SN2s{i}_", 2) for i in range(NSLOT)]
        XBs = [[sb(f"XBs{i}_{k}", [128, 128], BF16) for k in range(2)] for i in range(NSLOT)]
        bXBs = [bufs(f"XBs{i}_", 2) for i in range(NSLOT)]
        mdone = {"m": True}
        HSIG = {}
        G8R = sb("G8R", [128, 16]); bG8R = Buf("G8R")
        SQ = [sb(f"SQ{i}", [128, 512]) for i in range(3)]; bSQ = bufs("SQ", 3)
        JKA = LDT; bJKA = bLDT
        JKB = SQ[2]; bJKB = bSQ[2]

        def psv(bank, L):
            base = bank[0:L, 0:1]
            return bass.AP(base.tensor, base.offset, [list(base.ap[0]), [256, 2], [L, 2], [1, L]])

        def rwkv_head_gen(l, pr, h, L, slot, nlev):
            A, bA = PS[2 * slot], bPS[2 * slot]
            B, bB = PS[2 * slot + 1], bPS[2 * slot + 1]
            SAB, bSAB = SABs[slot], bSABs[slot]
            SN2, bSN2 = SN2s[slot], bSN2s[slot]
            XB, bXB = XBs[slot], bXBs[slot]
            j, hh = h // 2, h % 2
            prt = slice(hh * 64, hh * 64 + 64)
            cr = slice(h * 64, h * 64 + 64)
            arT = AR[prt, j, :, 0:L]
            mm(A[0:L, 0:2 * L].rearrange("p (a t) -> p a t", a=2), BTt[prt, j, 0:L], arT, r=[bBK, bAR], w=[bA], inc=False)
            mm(A[0:L, 256:256 + 2 * L].rearrange("p (a t) -> p a t", a=2), KTt[prt, j, 0:L], arT, r=[bBK, bAR], w=[bA])
            mm(B[0:L, 0:L], AR[prt, j, 0, 0:L], BTt[prt, j, 0:L], r=[bBK, bAR], w=[bB])
            yield
            P.op("dve", lambda e: e.tensor_tensor(SAB[0:L, :, :, 0:L], psv(A, L), M2[0:L, :, 0:L].unsqueeze(1).to_broadcast([L, 2, 2, L]), ALU.mult),
                 reads=[bA, bM2], writes=[bSAB])
            P.op("dve", lambda e: e.tensor_tensor(SN2[0][0:L, 1, 0:L], B[0:L, 0:L], cs(C_LOS, L), ALU.mult), reads=[bB, bCT], writes=[bSN2[0]])
            yield
            mm(B[0:L, 128:192], SAB[0:L, 1, 0, 0:L], TOK[0:L, 3, cr], r=[bSAB, bTOK], w=[bB])
            P.op("act", lambda e: e.copy(XB[0][0:L, 64:128], TOK[0:L, 0, cr]), reads=[bTOK], writes=[bXB[0]])
            yield
            P.op("act", lambda e: e.copy(XB[0][0:L, 0:64], B[0:L, 128:192]), reads=[bB], writes=[bXB[0]])
            yield
            cur_ = 0
            nt_ap, bnt = SAB[0:L, 0, 0, 0:L], bSAB
            n_ap, bn = SN2[0][0:L, 1, 0:L], bSN2[0]
            for lev in range(nlev):
                nx = 1 - cur_
                mm(B[0:L, 256:384], nt_ap, XB[cur_][0:L, :], r=[bnt, bXB[cur_]], w=[bB])
                if lev < nlev - 1:
                    mm(A[0:L, 0:L], n_ap, nt_ap, r=[bn, bnt], w=[bA], inc=False)
                    mm(A[0:L, 128:128 + L], nt_ap, n_ap, r=[bn, bnt], w=[bA])
                yield
                P.op("dve", lambda e, cur_=cur_, nx=nx: e.tensor_tensor(XB[nx][0:L, :], XB[cur_][0:L, :], B[0:L, 256:384], ALU.add),
                     reads=[bB, bXB[cur_]], writes=[bXB[nx]])
                if lev < nlev - 1:
                    evac(SN2[nx][0:L, :, 0:L], A[0:L, 0:256].rearrange("p (a t) -> p a t", a=2)[:, :, 0:L], r=[bA], w=[bSN2[nx]], eng="act")
                    nt_ap, bnt = SN2[nx][0:L, 0, 0:L], bSN2[nx]
                    n_ap, bn = SN2[nx][0:L, 1, 0:L], bSN2[nx]
                yield
                cur_ = nx
            XF, bXF = XB[cur_], bXB[cur_]
            mm(A[0:L, 256:320], SAB[0:L, 0, 1, 0:L], XF[0:L, 0:64], r=[bSAB, bXF], w=[bA], start=True, stop=False, inc=False)
            mm(A[0:L, 256:320], SAB[0:L, 1, 1, 0:L], TOK[0:L, 3, cr], r=[bSAB, bTOK], w=[bA], start=False, stop=True, inc=False)
            mm(A[0:64, 320:320 + L], XF[0:L, 64:128], SAB[0:L, 0, 1, 0:L], r=[bXF, bSAB], w=[bA], start=True, stop=False, inc=False)
            mm(A[0:64, 320:320 + L], IDB[prt, hh * 64:hh * 64 + 64], AR[prt, j, 1, 0:L], r=[bIDB, bAR], w=[bA], start=False, stop=True)
            mm(B[0:64, 0:64], CT[prt, C_ID + hh * 64:C_ID + hh * 64 + 64], DG[prt, j, :], r=[bCT, bDG], w=[bB], start=True, stop=False, inc=False)
            mm(B[0:64, 0:64], XF[0:L, 64:128], TOK[0:L, 1, cr], r=[bXF, bTOK], w=[bB], start=False, stop=True, inc=False)
            mm(B[0:64, 64:128], TOK[0:L, 1, cr], XF[0:L, 0:64], r=[bTOK, bXF], w=[bB], start=True, stop=False, inc=False)
            mm(B[0:64, 64:128], TOK[0:L, 2, cr], TOK[0:L, 3, cr], r=[bTOK], w=[bB], start=False, stop=True)
            yield
            evac(Y0[0:L, cr], A[0:L, 256:320], r=[bA], w=[bY0], eng="act")
            evac(MT[:, h, :], B[0:64, 0:64], r=[bB], w=[bMT], eng="dve")
            evac(QT[:, h, 0:L], A[0:64, 320:320 + L], r=[bA], w=[bQT], eng="act")
            evac(ZS[:, cr], B[0:64, 64:128], r=[bB], w=[bZS], eng="dve")
            yield

        def rwkv_gen(l, pr, ci, off, L, seq, pc):
            nlev = int(np.log2(L))
            sl = slice(pc, pc + L)
            ol = slice(off, off + L)
            R = PS[0:4]
            bR = bPS[0:4]
            mm(R[0][0:L, :], TW[0:64, ol], pr["w2a2"][0:64, :], r=[bTW, bPRM], w=[bR[0]])
            yield
            P.op("dve", lambda e: e.tensor_tensor(LDT[0:L, :], R[0][0:L, :], pr["w0"][0:L, :], ALU.add), reads=[bR[0], bLW], writes=[bLDT])
            yield
            P.op("act", lambda e: e.activation(LDT[0:L, :], LDT[0:L, :], AF.Exp, scale=-1.0), reads=[bLDT], writes=[bLDT])
            yield
            P.op("act", lambda e: e.activation(LDT[0:L, :], LDT[0:L, :], AF.Ln, bias=1.0), reads=[bLDT], writes=[bLDT])
            yield
            P.op("act", lambda e: e.activation(LDT[0:L, :], LDT[0:L, :], AF.Exp, scale=-1.0), reads=[bLDT], writes=[bLDT])
            yield
            for j in range(4):
                mm(R[1][:, j * 128:j * 128 + L], LDT[0:L, j * 128:(j + 1) * 128], cs(C_TRIIW, L), r=[bLDT, bCT], w=[bR[1]], inc=False)
                mm(R[2][:, j * 128:j * 128 + L], LDT[0:L, j * 128:(j + 1) * 128], cs(C_TRISW, L), r=[bLDT, bCT], w=[bR[2]], inc=(j == 3))
            yield
            p0 = R[1][:].rearrange("p (j t) -> p j t", j=4)[:, :, 0:L]
            p1 = R[2][:].rearrange("p (j t) -> p j t", j=4)[:, :, 0:L]
            s3 = lambda i: S[i][:].rearrange("p (j t) -> p j t", j=4)[:, :, 0:L]
            P.op("dve", lambda e: e.tensor_copy(SM[:, 4:8].unsqueeze(2), p0[:, :, L - 1:L]), reads=[bR[1]], writes=[bSM])
            P.op("act", lambda e: e.activation(s3(1), p1, AF.Exp), reads=[bR[2]], writes=[bS[1]])
            P.op("act", lambda e: e.activation(s3(0), p0, AF.Exp), reads=[bR[1]], writes=[bS[0]])
            yield
            P.op("dve", lambda e: e.tensor_tensor(AR[:, :, 0, 0:L], NKK[:, :, ol], s3(1), ALU.mult), reads=[bNB, bS[1]], writes=[bAR])
            P.op("act", lambda e: e.activation(s3(2), p0, AF.Exp, scale=-1.0), reads=[bR[1]], writes=[bS[2]])
            yield
            P.op("dve", lambda e: e.tensor_tensor(AR[:, :, 1, 0:L], PT[:, 0:4, sl], s3(0), ALU.mult), reads=bPT[0:4] + [bS[0]], writes=[bAR])
            for j in range(4):
                P.op("act", lambda e, j=j: e.activation(S[3][:, j * 128:j * 128 + L], R[1][:, j * 128:j * 128 + L], AF.Exp, bias=SM[:, 4 + j:5 + j], scale=-1.0),
                     reads=[bR[1], bSM], writes=[bS[3]])
            yield
            P.op("dve", lambda e: e.tensor_tensor(BTt[:, :, 0:L], BBT[:, :, ol], s3(2), ALU.mult), reads=[bNB, bS[2]], writes=[bBK])
            P.op("dve", lambda e: e.tensor_tensor(KTt[:, :, 0:L], PT[:, 4:8, sl], s3(2), ALU.mult), reads=bPT[4:8] + [bS[2]], writes=[bBK])
            P.op("act", lambda e: e.copy(VBT[:, :, 0:L], PT[:, 8:12, sl]), reads=bPT[8:12], writes=[bBKH])
            yield
            P.op("dve", lambda e: e.tensor_tensor(BHT[:, :, 0:L], BBT[:, :, ol], s3(3), ALU.mult), reads=[bNB, bS[3]], writes=[bBKH])
            P.op("dve", lambda e: e.tensor_tensor(KHT[:, :, 0:L], PT[:, 4:8, sl], s3(3), ALU.mult), reads=bPT[4:8] + [bS[3]], writes=[bBKH])
            P.op("dve", lambda e: e.tensor_tensor(DG[:], CT[:, C_ID2:C_ID2 + 64].unsqueeze(1).to_broadcast([128, 4, 64]),
                                                  s3(0)[:, :, L - 1:L].to_broadcast([128, 4, 64]), ALU.mult), reads=[bCT, bS[0]], writes=[bDG])
            yield
            if RW_STOP == 1:
                return
            for q, src in enumerate([AR, BHT, KHT, VBT]):
                bank = 0 if q < 2 else 3
                for j in range(4):
                    s_ap = (src[:, j, 0, 0:L] if q == 0 else src[:, j, 0:L])
                    P.op("pe", lambda e, s_ap=s_ap, q=q, j=j, bank=bank: e.transpose(PSB[bank][0:L, (q % 2) * 512 + j * 128:(q % 2) * 512 + (j + 1) * 128], s_ap, IDB[:, :]),
                         reads=[bAR if q == 0 else bBKH, bIDB], writes=[bR[bank]], inc=(j == 3))
            yield
            evac(TOK[0:L, 0:2, :], PSB[0][0:L, :].rearrange("p (q c) -> p q c", q=2), r=[bR[0]], w=[bTOK])
            evac(TOK[0:L, 2:4, :], PSB[3][0:L, :].rearrange("p (q c) -> p q c", q=2), r=[bR[3]], w=[bTOK])
            if RW_STOP == 2:
                return
            for j in range(4):
                P.op("dve", lambda e, j=j: e.scalar_tensor_tensor(TMPA[:, 0:L], PT[:, j, sl], pr["rk"][:, j:j + 1], PT[:, 4 + j, sl], ALU.mult, ALU.mult),
                     reads=[bPT[j], bPT[4 + j], bPRM], writes=[bTA])
                mm(R[2][0:L, 500 + 2 * j:502 + 2 * j], TMPA[:, 0:L], CT[:, C_HSEL:C_HSEL + 2], r=[bTA, bCT], w=[bR[2]])
                yield
            P.op("dve", lambda e: e.tensor_copy(SM[0:L, 8:16], R[2][0:L, 500:508]), reads=[bR[2]], writes=[bSM])
            yield
            if RW_STOP == 3:
                return
            todo = list(range(8))
            free = [0, 1]
            extra = False
            live = []
            while todo or live:
                if (not extra) and mdone["m"]:
                    extra = True
                    free += [2, 3]
                while todo and free:
                    sl_ = free.pop(0)
                    live.append((rwkv_head_gen(l, pr, todo.pop(0), L, sl_, nlev), sl_))
                nl = []
                for g_, sl_ in live:
                    try:
                        next(g_)
                        nl.append((g_, sl_))
                    except StopIteration:
                        free.append(sl_)
                live = nl
                yield
            HSIG[ci] = True
            Hc = Hs[l][seq]
            bHc = bH[l][seq]
            P.op("act", lambda e: e.copy(HBF[:, :], Hc[:, :]), reads=[bHc], writes=[bHBF])
            for h in range(8):
                cr = slice(h * 64, h * 64 + 64)
                mm(R[1][0:64, cr], MT[:, h, :], Hc[:, cr], r=[bMT, bHc], w=[bR[1]], inc=(h == 7))
            mm(R[2][0:L, :], SG[:, ol], pr["g2w"][:, :], r=[bSG, bPRM], w=[bR[2]])
            yield
            for h in range(8):
                cr = slice(h * 64, h * 64 + 64)
                mm(R[0][0:L, cr], QT[:, h, 0:L], HBF[:, cr], r=[bQT, bHBF], w=[bR[0]], inc=(h == 7))
            yield
            P.op("dve", lambda e: e.tensor_tensor(Hc[:, :], ZS[:, :], R[1][0:64, :], ALU.add), reads=[bZS, bR[1]], writes=[bHc])
            P.op("dve", lambda e: e.tensor_tensor(Y0[0:L, :], Y0[0:L, :], R[0][0:L, :], ALU.add), reads=[bY0, bR[0]], writes=[bY0])
            dump("y0pre", l, ci, Y0[0:L, :], F32, [bY0])
            dump("hnew", l, ci, Hc[:, :], F32, [bHc])
            yield
            y3 = Y0[0:L, :].rearrange("p (h c) -> p h c", h=8)
            st8 = lambda a, b_: G8R[0:L, a:b_]
            P.op("dve", lambda e: e.tensor_reduce(st8(0, 8), y3, AX.X, ALU.add), reads=[bY0], writes=[bG8R])
            yield
            P.op("dve", lambda e: e.tensor_scalar(st8(0, 8), st8(0, 8), 1.0 / 64, None, ALU.mult), reads=[bG8R], writes=[bG8R])
            yield
            P.op("dve", lambda e: e.tensor_tensor(y3, y3, st8(0, 8).unsqueeze(2).to_broadcast([L, 8, 64]), ALU.subtract), reads=[bY0, bG8R], writes=[bY0])
            yield
            j3 = JKA[0:L, 0:512].rearrange("p (h c) -> p h c", h=8)
            P.op("dve", lambda e: e.tensor_tensor(j3, y3, y3, ALU.mult), reads=[bY0], writes=[bJKA])
            yield
            P.op("dve", lambda e: e.tensor_reduce(st8(8, 16), j3, AX.X, ALU.add), reads=[bJKA], writes=[bG8R])
            yield
            P.op("act", lambda e: e.activation(st8(8, 16), st8(8, 16), AF.Ln, bias=64e-5, scale=1.0 / 64), reads=[bG8R], writes=[bG8R])
            yield
            P.op("act", lambda e: e.activation(st8(8, 16), st8(8, 16), AF.Exp, scale=-0.5), reads=[bG8R], writes=[bG8R])
            P.op("dve", lambda e: e.tensor_tensor(j3, TOK[0:L, 3, :].rearrange("p (h c) -> p h c", h=8), SM[0:L, 8:16].unsqueeze(2).to_broadcast([L, 8, 64]), ALU.mult),
                 reads=[bTOK, bSM], writes=[bJKA])
            yield
            P.op("dve", lambda e: e.tensor_tensor(y3, y3, st8(8, 16).unsqueeze(2).to_broadcast([L, 8, 64]), ALU.mult), reads=[bY0, bG8R], writes=[bY0])
            yield
            P.op("dve", lambda e: e.tensor_tensor(Y0[0:L, :], Y0[0:L, :], pr["lw"][0:L, :], ALU.mult), reads=[bY0, bLW], writes=[bY0])
            yield
            P.op("dve", lambda e: e.tensor_tensor(Y0[0:L, :], Y0[0:L, :], pr["lb"][0:L, :], ALU.add), reads=[bY0, bLW], writes=[bY0])
            yield
            P.op("dve", lambda e: e.tensor_tensor(Y0[0:L, :], Y0[0:L, :], JKA[0:L, 0:512], ALU.add), reads=[bY0, bJKA], writes=[bY0])
            yield
            P.op("dve", lambda e: e.tensor_tensor(YBb[0:L, 0:512], Y0[0:L, :], R[2][0:L, :], ALU.mult), reads=[bY0, bR[2]], writes=[bYA])
            yield

        def mlstm_gen(l, pr, ci, off, L, seq, wv_vo, wgt, guard=lambda: True):
            ol = slice(off, off + L)
            M = PS[4:8]
            bM = bPS[4:8]
            for g in range(2):
                wv, bw = wv_vo[g]
                for kc in range(8):
                    mm(M[g][0:L, :], HT[:, kc, ol], wv[:, kc, :], r=[bHT, bw], w=[bM[g]], start=(kc == 0), stop=(kc == 7), inc=(kc == 7))
                yield
            wg, bwg = wgt
            for kc in range(8):
                mm(M[2][0:L, 0:8], HT[:, kc, ol], wg[:, kc, :], r=[bHT, bwg], w=[bM[2]], start=(kc == 0), stop=(kc == 7), inc=(kc == 7))
            yield
            P.op("act", lambda e: e.copy(VE[0:L, :, 0:128], M[0][0:L, :].rearrange("p (h e) -> p h e", h=4)), reads=[bM[0]], writes=[bVE])
            P.op("pool", lambda e: e.memset(VE[0:L, :, 128:129], 1.0), writes=[bVE])
            g8 = lambda a, b_: G8[0:L, a:b_]
            P.op("dve", lambda e: e.tensor_tensor(g8(16, 24), M[2][0:L, 0:8], pr["ifb"][0:L, :], ALU.add), reads=[bM[2], bPRM], writes=[bG8])
            yield
            P.op("act", lambda e: e.activation(g8(20, 24), g8(20, 24), AF.Exp, scale=-1.0), reads=[bG8], writes=[bG8])
            yield
            P.op("act", lambda e: e.activation(g8(20, 24), g8(20, 24), AF.Ln, bias=1.0), reads=[bG8], writes=[bG8])
            yield
            P.op("dve", lambda e: e.tensor_scalar(g8(20, 24), g8(20, 24), -1.0, None, ALU.mult), reads=[bG8], writes=[bG8])
            P.op("act", lambda e: e.activation(VO[0:L, 0:512], M[1][0:L, :], AF.Exp, scale=-1.0), reads=[bM[1]], writes=[bVO])
            yield
            P.op("act", lambda e: e.activation(VO[0:L, 0:512], VO[0:L, 0:512], AF.Ln, bias=1.0), reads=[bVO], writes=[bVO])
            yield
            P.op("act", lambda e: e.activation(VO[0:L, 0:512], VO[0:L, 0:512], AF.Exp, scale=-1.0), reads=[bVO], writes=[bVO])
            yield
            mm(M[2][0:L, 16:20], cs(C_TRII, L), g8(20, 24), r=[bCT, bG8], w=[bM[2]])
            yield
            P.op("dve", lambda e: e.tensor_copy(g8(24, 28), M[2][0:L, 16:20]), reads=[bM[2]], writes=[bG8])
            yield
            P.op("dve", lambda e: e.tensor_tensor(g8(28, 32), g8(16, 20), g8(24, 28), ALU.subtract), reads=[bG8], writes=[bG8])
            yield
            idb = lambda: CT[0:L, C_ID:C_ID + L].unsqueeze(1).to_broadcast([L, 4, L])
            s4 = lambda i: SQ[i][0:L, :].rearrange("p (h t) -> p h t", h=4)[:, :, 0:L]
            p4 = lambda i: M[i][0:L, :].rearrange("p (h t) -> p h t", h=4)[:, :, 0:L]
            P.op("dve", lambda e: e.tensor_tensor(s4(0), idb(), g8(28, 32).unsqueeze(2).to_broadcast([L, 4, L]), ALU.mult), reads=[bCT, bG8], writes=[bSQ[0]])
            yield
            for h in range(4):
                mm(M[3][0:L, h * 128:h * 128 + L], cs(C_ONES, L), SQ[0][0:L, h * 128:h * 128 + L], r=[bCT, bSQ[0]], w=[bM[3]], inc=(h == 3))
            yield
            P.op("dve", lambda e: e.tensor_tensor(s4(1), p4(3), g8(24, 28).unsqueeze(2).to_broadcast([L, 4, L]), ALU.add), reads=[bM[3], bG8], writes=[bSQ[1]])
            yield
            P.op("dve", lambda e: e.tensor_tensor(s4(1), s4(1), CT[0:L, C_MNEG:C_MNEG + L].unsqueeze(1).to_broadcast([L, 4, L]), ALU.add), reads=[bSQ[1], bCT], writes=[bSQ[1]])
            yield
            P.op("dve", lambda e: e.tensor_reduce(g8(32, 36), s4(1), AX.X, ALU.max), reads=[bSQ[1]], writes=[bG8])
            m0 = M0[l][seq]
            P.op("dve", lambda e: e.tensor_tensor(g8(36, 40), g8(24, 28), m0[0:L, :], ALU.add), reads=[bG8, bM0[l][seq]], writes=[bG8])
            yield
            P.op("dve", lambda e: e.tensor_tensor(g8(40, 44), g8(36, 40), g8(32, 36), ALU.max), reads=[bG8], writes=[bG8])
            yield
            P.op("dve", lambda e: e.tensor_tensor(g8(44, 48), g8(36, 40), g8(40, 44), ALU.subtract), reads=[bG8], writes=[bG8])
            yield
            P.op("dve", lambda e: e.tensor_tensor(g8(48, 52), g8(24, 28), g8(40, 44), ALU.subtract), reads=[bG8], writes=[bG8])
            P.op("act", lambda e: e.activation(g8(44, 48), g8(44, 48), AF.Exp), reads=[bG8], writes=[bG8])
            yield
            P.op("dve", lambda e: e.tensor_tensor(s4(0), idb(), g8(48, 52).unsqueeze(2).to_broadcast([L, 4, L]), ALU.mult), reads=[bCT, bG8], writes=[bSQ[0]])
            P.op("act", lambda e: e.activation(g8(52, 56), g8(40, 44), AF.Exp, scale=-1.0), reads=[bG8], writes=[bG8])
            yield
            for h in range(4):
                mm(M[3][0:L, h * 128:h * 128 + L], cs(C_ONES, L), SQ[0][0:L, h * 128:h * 128 + L], r=[bCT, bSQ[0]], w=[bM[3]], inc=(h == 3))
            yield
            P.op("dve", lambda e: e.tensor_tensor(s4(2), p4(3), CT[0:L, C_MTNEG:C_MTNEG + L].unsqueeze(1).to_broadcast([L, 4, L]), ALU.add), reads=[bM[3], bCT], writes=[bSQ[2]])
            yield
            for h in range(4):
                P.op("act", lambda e, h=h: e.activation(SQ[2][0:L, h * 128:h * 128 + L], SQ[2][0:L, h * 128:h * 128 + L], AF.Exp, bias=G8[0:L, 28 + h:29 + h]),
                     reads=[bSQ[2], bG8], writes=[bSQ[2]])
            for h in range(4):
                mm(M[3][0:L, h * 128:h * 128 + L], QKT[:, 4 + h, ol], QKT[:, h, ol], r=[bQK], w=[bM[3]], inc=(h == 3))
            yield
            Cc = CE[l][seq]
            bCc = bCE[l][seq]
            P.op("act", lambda e: e.copy(CB[:], Cc[:]), reads=[bCc], writes=[bCB])
            for h in range(4):
                P.op("pe", lambda e, h=h: e.transpose(PSB[6][0:L, h * 128:(h + 1) * 128], QKT[:, 4 + h, ol], IDB[:, :]), reads=[bQK, bIDB], writes=[bM[2]], inc=(h == 3))
            yield
            P.op("dve", lambda e: e.tensor_tensor(STb[0:L, :].rearrange("p (h t) -> p h t", h=4)[:, :, 0:L], p4(3), s4(2), ALU.mult), reads=[bM[3], bSQ[2]], writes=[bSTb])
            evac(KTOK[0:L, :], PSB[6][0:L, 0:512], r=[bM[2]], w=[bKTOK])
            yield
            for h in range(4):
                pa = M[h // 2][0:L, (h % 2) * 129:(h % 2) * 129 + 129]
                pq = M[2 + h // 2][0:L, (h % 2) * 129:(h % 2) * 129 + 129]
                mm(pa, STb[0:L, h * 128:h * 128 + L], VE[0:L, h, :], r=[bSTb, bVE], w=[bM[h // 2]])
                mm(pq, QKT[:, h, ol], CB[:, h, :], r=[bQK, bCB], w=[bM[2 + h // 2]])
            yield
            for hp in range(2):
                nd = ND[0:L, 2 * hp:2 * hp + 2, :]
                P.op("dve", lambda e, hp=hp, nd=nd: e.tensor_tensor(nd, M[2 + hp][0:L, 0:258].rearrange("p (h e) -> p h e", h=2),
                                                                    G8[0:L, 44 + 2 * hp:46 + 2 * hp].unsqueeze(2).to_broadcast([L, 2, 129]), ALU.mult),
                     reads=[bM[2 + hp], bG8], writes=[bND])
            yield
            for hp in range(2):
                nd = ND[0:L, 2 * hp:2 * hp + 2, :]
                P.op("dve", lambda e, hp=hp, nd=nd: e.tensor_tensor(nd, nd, M[hp][0:L, 0:258].rearrange("p (h e) -> p h e", h=2), ALU.add),
                     reads=[bM[hp], bND], writes=[bND])
            yield
            el = CT[0:L, (C_EL128 if L == 128 else C_EL16):(C_EL128 if L == 128 else C_EL16) + 128]
            P.op("dve", lambda e: e.tensor_copy(G8[0:L, 0:4], g8(40, 44)), reads=[bG8], writes=[bG8])
            P.op("dve", lambda e: e.tensor_copy(G8[0:L, 4:8], g8(24, 28)), reads=[bG8], writes=[bG8])
            yield
            mm(M[2][:, 300:308], el, G8[0:L, 0:8], r=[bCT, bG8], w=[bM[2]])
            P.op("dve", lambda e: e.tensor_scalar(g8(56, 60), ND[0:L, :, 128], -1.0, None, ALU.mult), reads=[bND], writes=[bG8])
            yield
            P.op("dve", lambda e: e.tensor_tensor(g8(56, 60), g8(56, 60), ND[0:L, :, 128], ALU.max), reads=[bND, bG8], writes=[bG8])
            P.op("act", lambda e: e.copy(SMB[:, 0:8], M[2][:, 300:308]), reads=[bM[2]], writes=[bSMB])
            yield
            P.op("dve", lambda e: e.tensor_tensor(g8(56, 60), g8(56, 60), g8(52, 56), ALU.max), reads=[bG8], writes=[bG8])
            yield
            P.op("dve", lambda e: e.reciprocal(g8(56, 60), g8(56, 60)), reads=[bG8], writes=[bG8])
            P.op("dve", lambda e: e.tensor_tensor(SMB[:, 8:12], SMB[:, 4:8], SMB[:, 0:4], ALU.subtract), reads=[bSMB], writes=[bSMB])
            yield
            h3 = JKB[0:L, 0:512].rearrange("p (h e) -> p h e", h=4)
            P.op("dve", lambda e: e.tensor_tensor(h3, ND[0:L, :, 0:128], g8(56, 60).unsqueeze(2).to_broadcast([L, 4, 128]), ALU.mult), reads=[bND, bG8], writes=[bJKB])
            P.op("dve", lambda e: e.tensor_tensor(g8(8, 12), g8(28, 32), SMB[0:L, 8:12], ALU.add), reads=[bG8, bSMB], writes=[bG8])
            yield
            q3 = SQ[1][0:L, :].rearrange("p (h e) -> p h e", h=4)
            P.op("dve", lambda e: e.tensor_tensor(q3, h3, h3, ALU.mult), reads=[bJKB], writes=[bSQ[1]])
            P.op("act", lambda e: e.activation(g8(8, 12), g8(8, 12), AF.Exp), reads=[bG8], writes=[bG8])
            P.op("dve", lambda e: e.tensor_tensor(SMB[:, 12:16], SMB[:, 8:12], m0[:, :], ALU.add), reads=[bSMB, bM0[l][seq]], writes=[bSMB])
            yield
            P.op("dve", lambda e: e.tensor_reduce(g8(60, 64), q3, AX.X, ALU.add), reads=[bSQ[1]], writes=[bG8])
            P.op("act", lambda e: e.activation(SMB[:, 12:16], SMB[:, 12:16], AF.Exp), reads=[bSMB], writes=[bSMB])
            yield
            P.op("dve", lambda e: e.tensor_tensor(KW[0:L, :].rearrange("p (h d) -> p h d", h=4), KTOK[0:L, :].rearrange("p (h d) -> p h d", h=4),
                                                  g8(8, 12).unsqueeze(2).to_broadcast([L, 4, 128]), ALU.mult), reads=[bKTOK, bG8], writes=[bKW])
            P.op("act", lambda e: e.activation(g8(60, 64), g8(60, 64), AF.Ln, bias=1e-6, scale=1.0 / 128), reads=[bG8], writes=[bG8])
            yield
            P.op("dve", lambda e: e.tensor_copy(m0[:, :], SMB[:, 0:4]), reads=[bSMB], writes=[bM0[l][seq]])
            P.op("act", lambda e: e.activation(g8(60, 64), g8(60, 64), AF.Exp, scale=-0.5), reads=[bG8], writes=[bG8])
            for h in range(4):
                mm(M[h // 2][:, (h % 2) * 129:(h % 2) * 129 + 129], KW[0:L, h * 128:(h + 1) * 128], VE[0:L, h, :], r=[bKW, bVE], w=[bM[h // 2]])
            yield
            P.op("dve", lambda e: e.tensor_tensor(Cc[:], Cc[:], SMB[:, 12:16].unsqueeze(2).to_broadcast([128, 4, 129]), ALU.mult), reads=[bCc, bSMB], writes=[bCc])
            yield
            P.op("dve", lambda e: e.tensor_tensor(h3, h3, g8(60, 64).unsqueeze(2).to_broadcast([L, 4, 128]), ALU.mult), reads=[bJKB, bG8], writes=[bJKB])
            yield
            for hp in range(2):
                P.op("dve", lambda e, hp=hp: e.tensor_tensor(Cc[:, 2 * hp:2 * hp + 2, :], Cc[:, 2 * hp:2 * hp + 2, :],
                                                             M[hp][:, 0:258].rearrange("p (h e) -> p h e", h=2), ALU.add),
                     reads=[bM[hp], bCc], writes=[bCc])
            yield
            P.op("dve", lambda e: e.tensor_tensor(JKB[0:L, 0:512], JKB[0:L, 0:512], pr["hw"][0:L, :], ALU.mult), reads=[bJKB, bLW], writes=[bJKB])
            yield
            while not guard():
                yield
            P.op("dve", lambda e: e.tensor_tensor(YBb[0:L, 512:1024], JKB[0:L, 0:512], VO[0:L, 0:512], ALU.mult), reads=[bJKB, bVO], writes=[bYA])
            yield

        SMB = sb("SMB", [128, 16]); bSMB = Buf("SMB")

        for sti in range(NST + 1):
            cur["sti"] = sti
            process_st(sti)
        P.finish("sp")
        LAST["count"] = dict(P.count)
        LAST["n"] = {e: len(v) for e, v in P.streams.items()}
        P.emit()
    return nc


_WNAMES = ["norm_mix", "w_in", "a_mu", "a_w0", "a_w2", "a_a0", "a_a2", "a_g2", "a_k_k", "a_k_a", "a_r_k", "a_ln_w", "a_ln_b",
           "b_conv_w", "b_conv_b", "b_i_bias", "b_f_bias", "b_hn_w", "w_out", "norm_ffn", "w_up", "ffn_conv_w", "ffn_conv_b",
           "w_down", "norm_final"]


def run(inputs, NST, NCH=2, ncores=8):
    f = lambda a: np.ascontiguousarray(np.asarray(a, dtype=np.float32))
    nc = build_program(NST, NCH)
    cst = make_consts()
    nb = inputs["x_prompt"].shape[0]
    in_maps = []
    for c in range(ncores):
        b = c % nb
        s0 = (2 * c) % inputs["x_sample"].shape[0]
        m = {"xp": f(inputs["x_prompt"][b]), "meta": f(inputs["meta_tokens"]), "xs": f(inputs["x_sample"][s0:s0 + 2]),
             "st_shift": f(inputs["state_rwkv_shift"][:, s0:s0 + 2]), "st_wkv": f(inputs["state_rwkv_wkv"][:, s0:s0 + 2]),
             "st_conv": f(inputs["state_mlstm_conv"][:, s0:s0 + 2]), "st_C": f(inputs["state_mlstm_C"][:, s0:s0 + 2]),
             "st_n": f(inputs["state_mlstm_n"][:, s0:s0 + 2]), "st_m": f(inputs["state_mlstm_m"][:, s0:s0 + 2]),
             "st_fconv": f(inputs["state_ffn_conv"][:, s0:s0 + 2]), "consts": cst}
        for n in _WNAMES:
            m[n] = f(inputs[n])
        in_maps.append(m)
    res = run_bass_kernel_spmd(nc, in_maps, core_ids=list(range(ncores)))
    return res.results


def assemble(rs, nb, nsb):
    ncores = len(rs)
    y_prompt = np.stack([rs[b]["y_p"] for b in range(nb)])
    y_sample = np.concatenate([rs[c]["y_s"] for c in range(nsb // 2)], axis=0)
    outs = [y_prompt, y_sample]
    keys = ["o_shift", "o_wkv", "o_conv", "o_C", "o_n", "o_m", "o_fconv"]
    for k in keys:
        outs.append(np.stack([rs[b][k][:, 0] for b in range(nb)], axis=1))
    for k in keys:
        outs.append(np.concatenate([rs[c][k][:, 1:3] for c in range(nsb // 2)], axis=1))
    return tuple(np.ascontiguousarray(o, dtype=np.float32) for o in outs)


def kernel(**inputs):
    rs = run(inputs, NST=4096 // 256, NCH=2, ncores=8)
    return assemble(rs, 4, 16)
```

```python
import numpy as np
from contextlib import ExitStack
import concourse.bass as bass
import concourse.mybir as mybir
from concourse.bass_utils import run_bass_kernel_spmd

F32 = mybir.dt.float32
BF16 = mybir.dt.bfloat16
ALU = mybir.AluOpType
AF = mybir.ActivationFunctionType
AX = mybir.AxisListType

EPOCH = 16000
STRICT_WAR = False
TRANSITIVE = True
EMBED_WAIT = True
EMBED_ENG = ("pe", "act", "dve")
SEQ_THREADS = False
FLAG_OLDFFN = False
POOL_ENG = "dve"
RW_STOP = 0
SEQ_HEADS = False
SKIP_RWKV = False
SKIP_MLSTM = False
NDSEM = 20

D = 1024
DEPTH = 2
A_COLS = 1792
B_COLS = 2056
D_IN = 3848
D_FF = 2816
NFT = 22
NEG = -1.0e30
EXPM05 = 0.6065306597126334


class Buf:
    __slots__ = ("name", "w", "r", "psum")

    def __init__(self, name, psum=False):
        self.name = name
        self.w = None
        self.r = []
        self.psum = psum


def bufs(name, n):
    return [Buf(f"{name}{i}") for i in range(n)]


class _Rec:
    def __init__(self):
        self.call = None

    def __getattr__(self, name):
        def f(*a, **k):
            self.call = (name, a, k)
            return None
        return f


class Prog:
    ENG = ("pe", "act", "dve", "pool", "sp")

    def __init__(self, nc, stack, nepoch=8):
        self.nc = nc
        self.streams = {e: [] for e in self.ENG}
        self.count = {e: 0 for e in self.ENG}
        self.sems = {e: [stack.enter_context(nc.semaphore(f"s_{e}{i}")) for i in range(nepoch)]
                     for e in self.ENG}
        self.dsems = {q: [stack.enter_context(nc.semaphore(f"d_{q}{i}")) for i in range(NDSEM)]
                      for q in ("sp", "pool", "act")}
        self.dval = {q: [0] * NDSEM for q in self.dsems}
        self.dnext = {q: 0 for q in self.dsems}
        self.seen = {e: {} for e in self.ENG}
        self.know = {}
        self.nepoch = nepoch

    def _need(self, eng, ev, waits):
        if ev is None:
            return
        if ev[0] == "e":
            key = ("e", ev[1])
            v = ev[2]
        else:
            key = ("d", ev[1], ev[2])
            v = ev[3]
        if self.seen[eng].get(key, 0) >= v:
            return
        self.seen[eng][key] = v
        waits[key] = max(waits.get(key, 0), v)
        if TRANSITIVE:
            kn = self.know.get(ev)
            if kn:
                sn = self.seen[eng]
                for k2, v2 in kn.items():
                    if sn.get(k2, 0) < v2:
                        sn[k2] = v2
                        if k2 in waits and waits[k2] <= v2 and k2 != key:
                            del waits[k2]

    def _resolve(self, waits):
        out = []
        for key, v in waits.items():
            if key[0] == "e":
                ep = (v - 1) // EPOCH
                assert ep < self.nepoch, "too many instructions for semaphore epochs"
                out.append((self.sems[key[1]][ep], (v - 1) % EPOCH + 1))
            else:
                out.append((self.dsems[key[1]][key[2]], v))
        return out

    def op(self, eng, fn, reads=(), writes=(), inc=True):
        rec = _Rec()
        fn(rec)
        nm_, a_, k_ = rec.call
        fn = (lambda e, nm_=nm_, a_=a_, k_=k_: getattr(e, nm_)(*a_, **k_))
        waits = {}
        for b in reads:
            ev = b.w
            if ev is not None and not (eng == "pe" and ev[0] == "e" and ev[1] == "pe"):
                self._need(eng, ev, waits)
            if b.psum:
                for ev in b.r:
                    if ev[0] == "e" and ev[1] == eng:
                        continue
                    self._need(eng, ev, waits)
        for b in writes:
            ev = b.w
            if ev is not None and not (eng == "pe" and ev[0] == "e" and ev[1] == "pe"):
                self._need(eng, ev, waits)
            for ev in b.r:
                if ev[0] == "e" and ev[1] == eng and (eng == "pe" or not STRICT_WAR):
                    continue
                self._need(eng, ev, waits)
        gc = self.count[eng] + 1
        if inc:
            self.count[eng] = gc
        myev = ("e", eng, gc)
        if TRANSITIVE:
            kn = dict(self.seen[eng])
            kn[("e", eng)] = max(kn.get(("e", eng), 0), gc)
            self.know[myev] = kn
        for b in writes:
            b.w = myev
            b.r = []
        for b in reads:
            if b not in writes:
                b.r = [e for e in b.r if not (e[0] == "e" and e[1] == eng)] + [myev]
        ep = (gc - 1) // EPOCH
        assert ep < self.nepoch
        self.streams[eng].append((self._resolve(waits), fn, (self.sems[eng][ep], 1) if inc else None))

    def dma(self, q, out, in_, reads=(), writes=(), **kw):
        waits = {}
        for b in reads:
            self._need(q, b.w, waits)
        for b in writes:
            self._need(q, b.w, waits)
            for ev in b.r:
                self._need(q, ev, waits)
        idx = self.dnext[q]
        self.dnext[q] = (idx + 1) % NDSEM
        prev = self.dval[q][idx]
        if prev > 0:
            self._need(q, ("d", q, idx, prev), waits)
        val = prev + 16
        self.dval[q][idx] = val
        myev = ("d", q, idx, val)
        if TRANSITIVE:
            self.know[myev] = dict(self.seen[q])
        for b in writes:
            b.w = myev
            b.r = []
        for b in reads:
            b.r = b.r + [myev]
        fn = (lambda e, out=out, in_=in_, kw=kw: e.dma_start(out=out, in_=in_, **kw))
        self.streams[q].append((self._resolve(waits), fn, (self.dsems[q][idx], 16)))

    def finish(self, eng):
        waits = {}
        for q in self.dsems:
            for idx in range(NDSEM):
                if self.dval[q][idx] > 0:
                    self._need(eng, ("d", q, idx, self.dval[q][idx]), waits)
        self.streams[eng].append((self._resolve(waits), None, None))

    def emit(self):
        nc = self.nc
        with nc.Block() as block:
            def run(engh, name):
                for waits, fn, inc in self.streams[name]:
                    emb = None
                    if EMBED_WAIT and fn is not None and waits and name in EMBED_ENG:
                        emb = waits[-1]
                        waits = waits[:-1]
                    for sem, v in waits:
                        engh.wait_ge(sem, v)
                    if fn is not None:
                        ins = fn(engh)
                        if emb is not None:
                            ins._wait_ge(emb[0], emb[1])
                        if inc is not None:
                            ins.then_inc(inc[0], inc[1])

            @block.sync
            def _(e):
                run(e, "sp")

            @block.tensor
            def _(e):
                run(e, "pe")

            @block.scalar
            def _(e):
                run(e, "act")

            @block.vector
            def _(e):
                run(e, "dve")

            @block.gpsimd
            def _(e):
                run(e, "pool")


C_ID, C_TRII, C_TRIS, C_ONES, C_MNEG, C_MTNEG, C_LOS, C_BLK, C_HSEL, C_EL128, C_EL16, C_TRIIW, C_TRISW, C_ID2, C_END = (
    0, 128, 256, 384, 512, 640, 768, 896, 1024, 1026, 1154, 1282, 1410, 1538, 1602)


def make_consts():
    c = np.zeros((128, C_END), np.float32)
    i = np.arange(128)
    s = i[:, None]
    t = i[None, :]
    c[:, C_ID:C_ID + 128] = (s == t)
    c[:, C_TRII:C_TRII + 128] = (s <= t)
    c[:, C_TRIS:C_TRIS + 128] = (s < t)
    c[:, C_ONES:C_ONES + 128] = 1.0
    c[:, C_MNEG:C_MNEG + 128] = np.where(t <= s, 0.0, NEG)
    c[:, C_MTNEG:C_MTNEG + 128] = np.where(s <= t, 0.0, NEG)
    c[:, C_LOS:C_LOS + 128] = (s > t)
    c[:, C_BLK:C_BLK + 128] = ((s // 64) == (t // 64))
    c[:, C_HSEL] = (i < 64)
    c[:, C_HSEL + 1] = (i >= 64)
    c[127, C_EL128:C_EL128 + 128] = 1.0
    c[15, C_EL16:C_EL16 + 128] = 1.0
    c[:, C_TRIIW:C_TRIIW + 128] = -EXPM05 * (s <= t)
    c[:, C_TRISW:C_TRISW + 128] = -EXPM05 * (s < t)
    c[:, C_ID2:C_ID2 + 64] = ((s % 64) == i[None, :64])
    return c


LAST = {}
DBG = {}


def build_program(NST, NCH=2):
    NTOK = NST * NCH * 128
    NW = NCH * 128
    WP = NW + 8
    nc = bass.Bass("TRN2", target_bir_lowering=False)
    dram = lambda n, s, k, dt=F32: nc.dram_tensor(n, list(s), dt, kind=k).ap()
    I = "ExternalInput"
    O = "ExternalOutput"
    xp = dram("xp", [NTOK, D], I)
    meta = dram("meta", [16, D], I)
    xs = dram("xs", [2, 16, D], I)
    st_shift = dram("st_shift", [2, 2, A_COLS], I)
    st_wkv = dram("st_wkv", [2, 2, 8, 64, 64], I)
    st_conv = dram("st_conv", [2, 2, 3, 1024], I)
    st_C = dram("st_C", [2, 2, 4, 128, 128], I)
    st_n = dram("st_n", [2, 2, 4, 128], I)
    st_m = dram("st_m", [2, 2, 4], I)
    st_fconv = dram("st_fconv", [2, 2, 2, D_FF], I)
    consts = dram("consts", [128, C_END], I)
    W = {}
    for n, s in [("norm_mix", [2, D]), ("w_in", [2, D, D_IN]), ("a_mu", [2, A_COLS]), ("a_w0", [2, 512]),
                 ("a_w2", [2, 64, 512]), ("a_a0", [2, 512]), ("a_a2", [2, 64, 512]), ("a_g2", [2, 128, 512]),
                 ("a_k_k", [2, 512]), ("a_k_a", [2, 512]), ("a_r_k", [2, 512]), ("a_ln_w", [2, 512]),
                 ("a_ln_b", [2, 512]), ("b_conv_w", [2, 4, 1024]), ("b_conv_b", [2, 1024]), ("b_i_bias", [2, 4]),
                 ("b_f_bias", [2, 4]), ("b_hn_w", [2, 512]), ("w_out", [2, D, D]), ("norm_ffn", [2, D]),
                 ("w_up", [2, D, 2 * D_FF]), ("ffn_conv_w", [2, 3, D_FF]), ("ffn_conv_b", [2, D_FF]),
                 ("w_down", [2, D_FF, D]), ("norm_final", [D])]:
        W[n] = dram(n, s, I)
    y_p = dram("y_p", [NTOK, D], O)
    y_s = dram("y_s", [2, 16, D], O)
    o_shift = dram("o_shift", [2, 3, A_COLS], O)
    o_wkv = dram("o_wkv", [2, 3, 8, 64, 64], O)
    o_conv = dram("o_conv", [2, 3, 3, 1024], O)
    o_C = dram("o_C", [2, 3, 4, 128, 128], O)
    o_n = dram("o_n", [2, 3, 4, 128], O)
    o_m = dram("o_m", [2, 3, 4], O)
    o_fconv = dram("o_fconv", [2, 3, 2, D_FF], O)
    wb_in = dram("wb_in", [2, D, D_IN], "Internal", BF16)
    wb_out = dram("wb_out", [2, D, D], "Internal", BF16)
    wb_up = dram("wb_up", [2, D, 2 * D_FF], "Internal", BF16)
    wb_down = dram("wb_down", [2, D_FF, D], "Internal", BF16)
    b_scr = {}

    with ExitStack() as st:
        P = Prog(nc, st)
        def sb(n, s, dt=F32):
            nb = int(np.prod(s[1:])) * (2 if dt == BF16 else 4)
            LAST.setdefault("sbuf", []).append((n, nb))
            return st.enter_context(nc.sbuf_tensor(n, list(s), dt))
        pst = lambda n, s, dt=F32: st.enter_context(nc.psum_tensor(n, list(s), dt))

        cur = {"sti": -1}

        def dump(name, l, ci, ap, dt, rbufs):
            key = DBG.get(name)
            if key is None or key != (l, cur["sti"], ci):
                return
            o = dram("dbg_" + name, list(ap.shape), O, dt)
            P.dma("pool", o, ap, reads=rbufs)

        CT = sb("CT", [128, C_END]); bCT = Buf("CT")
        IDB = sb("IDB", [128, 128], BF16); bIDB = Buf("IDB")
        ONB = sb("ONB", [128, 128], BF16)
        P.dma("pool", CT[:], consts, writes=[bCT])
        P.op("dve", lambda e: e.tensor_copy(IDB[:], CT[:, C_ID:C_ID + 128]), reads=[bCT], writes=[bIDB])
        cs = lambda off, L, n=None: CT[0:L, off:off + (L if n is None else n)]

        PS = [pst(f"PS{i}", [128, 512]) for i in range(8)]
        bPS = [Buf(f"PS{i}", psum=True) for i in range(8)]
        PSB = [p[:].bitcast(BF16) for p in PS]

        _ev = [0]

        def evac(out, in_, r, w, eng=None):
            _ev[0] ^= 1
            if (eng == "act") or (eng is None and _ev[0]):
                P.op("act", lambda e: e.copy(out, in_), reads=r, writes=w)
            else:
                P.op("dve", lambda e: e.tensor_copy(out, in_), reads=r, writes=w)

        def mm(out, lhsT, rhs, r, w, start=True, stop=True, inc=True):
            P.op("pe", lambda e: e.matmul(out, lhsT, rhs, start=start, stop=stop), reads=r, writes=w, inc=inc)

        WSN = 4
        WSL = [sb(f"WS{i}", [128, 4096], BF16) for i in range(WSN)]; bWS = bufs("WS", WSN)
        STG = [WSL[i][:].bitcast(F32) for i in range(2)]
        bSTG = [bWS[0], bWS[1]]
        STB = [WSL[2][:, 0:2048], WSL[2][:, 2048:4096]]
        bSTB = bufs("STB", 2)
        k = 0
        for l in range(2):
            for (src, dst, rows, cols) in [(W["w_in"], wb_in, D, D_IN), (W["w_out"], wb_out, D, D),
                                           (W["w_up"], wb_up, D, 2 * D_FF), (W["w_down"], wb_down, D_FF, D)]:
                bsc_ = b_scr.setdefault((dst.tensor.name, l), Buf("scr"))
                for r0 in range(0, rows, 128):
                    for c0 in range(0, cols, 2048):
                        cw = min(2048, cols - c0)
                        i = k % 2
                        k += 1
                        P.dma("sp", STG[i][:, 0:cw], src[l, r0:r0 + 128, c0:c0 + cw], writes=[bSTG[i]])
                        h = cw // 2
                        P.op("act", lambda e, i=i, h=h: e.copy(STB[i][:, 0:h], STG[i][:, 0:h]),
                             reads=[bSTG[i]], writes=[bSTB[i]])
                        P.op("dve", lambda e, i=i, h=h, cw=cw: e.tensor_copy(STB[i][:, h:cw], STG[i][:, h:cw]),
                             reads=[bSTG[i]], writes=[bSTB[i]])
                        P.dma("pool", dst[l, r0:r0 + 128, c0:c0 + cw], STB[i][:, 0:cw], reads=[bSTB[i]], writes=[bsc_])

        def colload(name, src2d_T, ncol):
            t = sb(name, [128, ncol]); b = Buf(name)
            return t, b
        PRM = []
        bPRM = Buf("PRM")
        LWs = [sb(f"LW{i}", [128, 512]) for i in range(4)]
        bLW = Buf("LW")
        for l in range(2):
            d = {}
            d["mu"] = sb(f"mu{l}", [128, 14])
            d["kk"] = sb(f"kk{l}", [128, 4]); d["ka"] = sb(f"ka{l}", [128, 4]); d["rk"] = sb(f"rk{l}", [128, 4])
            d["a0"] = sb(f"a0{l}", [128, 4])
            d["cw"] = sb(f"cw{l}", [128, 4, 8]); d["cb"] = sb(f"cb{l}", [128, 8])
            d["fw"] = sb(f"fw{l}", [128, 3, NFT]); d["fb"] = sb(f"fb{l}", [128, NFT])
            d["g1"] = sb(f"g1{l}", [128, 8]); d["g2"] = sb(f"g2{l}", [128, 8])
            d["w2a2"] = sb(f"w2a2{l}", [128, 512], BF16); d["g2w"] = sb(f"g2w{l}", [128, 512], BF16)
            d["lw"] = LWs[0]; d["lb"] = LWs[1]; d["hw"] = LWs[2]; d["w0"] = LWs[3]
            d["ifb"] = sb(f"ifb{l}", [128, 8])
            PRM.append(d)
            ld = lambda dst, src: P.dma("pool", dst, src, writes=[bPRM], allow_slow_non_contiguous=True)
            ld(d["mu"][:], W["a_mu"][l].rearrange("(j p) -> p j", p=128))
            ld(d["kk"][:], W["a_k_k"][l].rearrange("(j p) -> p j", p=128))
            ld(d["ka"][:], W["a_k_a"][l].rearrange("(j p) -> p j", p=128))
            ld(d["rk"][:], W["a_r_k"][l].rearrange("(j p) -> p j", p=128))
            ld(d["a0"][:], W["a_a0"][l].rearrange("(j p) -> p j", p=128))
            for j in range(4):
                ld(d["cw"][:, j, :], W["b_conv_w"][l, j].rearrange("(j p) -> p j", p=128))
            ld(d["cb"][:], W["b_conv_b"][l].rearrange("(j p) -> p j", p=128))
            for j in range(3):
                ld(d["fw"][:, j, :], W["ffn_conv_w"][l, j].rearrange("(j p) -> p j", p=128))
            ld(d["fb"][:], W["ffn_conv_b"][l].rearrange("(j p) -> p j", p=128))
            ld(d["g1"][:], W["norm_mix"][l].rearrange("(j p) -> p j", p=128))
            ld(d["g2"][:], W["norm_ffn"][l].rearrange("(j p) -> p j", p=128))
            ld(d["ifb"][:, 0:4], W["b_i_bias"][l].partition_broadcast(128))
            ld(d["ifb"][:, 4:8], W["b_f_bias"][l].partition_broadcast(128))
            P.dma("pool", STG[0][0:64, 0:512], W["a_w2"][l], writes=[bSTG[0]])
            P.dma("pool", STG[0][64:128, 0:512], W["a_a2"][l], writes=[bSTG[0]])
            P.dma("pool", STG[0][:, 512:1024], W["a_g2"][l], writes=[bSTG[0]])
            P.op("dve", lambda e, d=d: e.tensor_copy(d["w2a2"][:], STG[0][:, 0:512]), reads=[bSTG[0]], writes=[bPRM])
            P.op("dve", lambda e, d=d: e.tensor_copy(d["g2w"][:], STG[0][:, 512:1024]), reads=[bSTG[0]], writes=[bPRM])
        GF = sb("GF", [128, 1024])
        P.dma("pool", GF[:], W["norm_final"].partition_broadcast(128), writes=[bPRM])

        NX = max(NCH, 3)
        X = sb("X", [128, NX, D]); bX = bufs("X", NX)
        HB = sb("HB", [128, D], BF16); bHB = Buf("HB")
        JK = sb("JK", [128, D]); bJK = Buf("JK")
        SM = sb("SM", [128, 16]); bSM = Buf("SM")
        HT = sb("HT", [128, 8, NW], BF16); bHT = Buf("HT")
        PT = sb("PT", [128, 14, WP]); bPT = bufs("PT", 14)
        assert 14 * WP * 4 >= NFT * NW * 2
        ZT = PT[:].rearrange("p j w -> p (j w)")[:, 0:NFT * NW // 2].bitcast(BF16).rearrange("p (f n) -> p f n", f=NFT)
        bZT = bufs("ZT", NFT)
        NKK = sb("NKK", [128, 4, NW], BF16); BBT = sb("BBT", [128, 4, NW], BF16); bNB = Buf("NB")
        TMPA = sb("TMPA", [128, NW]); bTA = Buf("TMPA")
        TMPB = sb("TMPB", [128, NW]); bTB = Buf("TMPB")
        TW = sb("TW", [128, NW], BF16); bTW = Buf("TW")
        SG = sb("SG", [128, NW], BF16); bSG = Buf("SG")
        QKT = sb("QKT", [128, 8, NW], BF16); bQK = Buf("QKT")
        S = [sb(f"S{i}", [128, 512]) for i in range(4)]; bS = bufs("S", 4)
        LDT = sb("LDT", [128, 512]); bLDT = Buf("LDT")
        AR = sb("AR", [128, 4, 2, 128], BF16); bAR = Buf("AR")
        BTt = sb("BTt", [128, 4, 128], BF16); KTt = sb("KTt", [128, 4, 128], BF16); bBK = Buf("BK")
        BHT = sb("BHT", [128, 4, 128], BF16); KHT = sb("KHT", [128, 4, 128], BF16); VBT = sb("VBT", [128, 4, 128], BF16)
        bBKH = Buf("BKH")
        TOK = sb("TOK", [128, 4, 512], BF16); bTOK = Buf("TOK")
        Y0 = sb("Y0", [128, 512]); bY0 = Buf("Y0")
        QT = sb("QT", [64, 8, 128], BF16); bQT = Buf("QT")
        MT = sb("MT", [64, 8, 64]); bMT = Buf("MT")
        ZS = sb("ZS", [64, 512]); bZS = Buf("ZS")
        DG = sb("DG", [128, 4, 64]); bDG = Buf("DG")
        Hs = [[sb(f"H{l}_{s}", [64, 512]) for s in range(3)] for l in range(2)]
        bH = [[Buf(f"H{l}_{s}") for s in range(3)] for l in range(2)]
        HBF = sb("HBF", [64, 512], BF16); bHBF = Buf("HBF")
        bYA = Buf("YA")
        YBb = sb("YBb", [128, 1024], BF16); bYB = Buf("YBb")
        YT = sb("YT", [128, 8, NW], BF16); bYT = Buf("YT")
        VO = sb("VO", [128, 512]); bVO = Buf("VO")
        VE = sb("VE", [128, 4, 129], BF16); bVE = Buf("VE")
        STb = sb("STb", [128, 512], BF16); bSTb = Buf("STb")
        KTOK = sb("KTOK", [128, 512], BF16); bKTOK = Buf("KTOK")
        KW = sb("KW", [128, 512], BF16); bKW = Buf("KW")
        CE = [[sb(f"CE{l}_{s}", [128, 4, 129]) for s in range(3)] for l in range(2)]
        bCE = [[Buf(f"CE{l}_{s}") for s in range(3)] for l in range(2)]
        CB = sb("CB", [128, 4, 129], BF16); bCB = Buf("CB")
        M0 = [[sb(f"M0{l}_{s}", [128, 4]) for s in range(3)] for l in range(2)]
        bM0 = [[Buf(f"M0{l}_{s}") for s in range(3)] for l in range(2)]
        ND = sb("ND", [128, 4, 129]); bND = Buf("ND")
        G8 = sb("G8", [128, 64]); bG8 = Buf("G8")
        U2 = [sb(f"U{i}", [128, WP]) for i in range(3)]; bU2 = bufs("U", 3)
        UA2 = [sb(f"UA{i}", [128, NW]) for i in range(3)]; bUA2 = bufs("UA", 3)
        CSH = [[sb(f"CSH{l}_{s}", [128, 14]) for s in range(3)] for l in range(2)]
        CCV = [[sb(f"CCV{l}_{s}", [128, 8, 3]) for s in range(3)] for l in range(2)]
        CFF = [[sb(f"CFF{l}_{s}", [128, NFT, 2]) for s in range(3)] for l in range(2)]
        bCAR = [[Buf(f"CAR{l}_{s}") for s in range(3)] for l in range(2)]
        ROW = sb("ROW", [4, 1536]); bROW = Buf("ROW")
        WG = sb("WG", [128, 64], BF16); bWG = Buf("WG")
        wsi = [0]

        def wslot():
            i = wsi[0] % WSN
            wsi[0] += 1
            return WSL[i], bWS[i]

        for l in range(2):
            P.op("pool", lambda e, l=l: e.memset(Hs[l][0][:], 0.0), writes=[bH[l][0]])
            P.op("pool", lambda e, l=l: e.memset(CE[l][0][:], 0.0), writes=[bCE[l][0]])
            P.op("pool", lambda e, l=l: e.memset(M0[l][0][:], 0.0), writes=[bM0[l][0]])
            P.op("pool", lambda e, l=l: e.memset(CSH[l][0][:], 0.0), writes=[bCAR[l][0]])
            P.op("pool", lambda e, l=l: e.memset(CCV[l][0][:], 0.0), writes=[bCAR[l][0]])
            P.op("pool", lambda e, l=l: e.memset(CFF[l][0][:], 0.0), writes=[bCAR[l][0]])
            for s in (1, 2):
                b = s - 1
                P.dma("pool", STG[1][0:64, 0:512].rearrange("p (h k) -> p h k", h=8),
                      st_wkv[l, b].rearrange("h v k -> v h k"), writes=[bSTG[1]])
                for h in range(8):
                    mm(PS[0][0:64, h * 64:(h + 1) * 64], STG[1][0:64, h * 64:(h + 1) * 64], CT[0:64, C_ID:C_ID + 64],
                       r=[bSTG[1], bCT], w=[bPS[0]])
                evac(Hs[l][s][:], PS[0][0:64, :], r=[bPS[0]], w=[bH[l][s]])
                P.dma("pool", CE[l][s][:, :, 0:128], st_C[l, b].rearrange("h d e -> d h e"), writes=[bCE[l][s]])
                P.dma("pool", CE[l][s][:, :, 128:129], st_n[l, b].rearrange("h (d o) -> d h o", o=1),
                      writes=[bCE[l][s]], allow_slow_non_contiguous=True)
                P.dma("pool", M0[l][s][:], st_m[l, b].partition_broadcast(128), writes=[bM0[l][s]])
                for (c0_, nt_) in ((0, 12), (1536, 2)):
                    P.dma("pool", ROW[0:1, 0:nt_ * 128], st_shift[l, b:b + 1, c0_:c0_ + nt_ * 128], writes=[bROW])
                    for j in range(nt_):
                        jj_ = c0_ // 128 + j
                        mm(PS[1][:, jj_:jj_ + 1], ROW[0:1, j * 128:(j + 1) * 128], CT[0:1, C_ID:C_ID + 1], r=[bROW, bCT], w=[bPS[1]])
                evac(CSH[l][s][:], PS[1][:, 0:14], r=[bPS[1]], w=[bCAR[l][s]])
                P.dma("pool", ROW[0:3, 0:1024], st_conv[l, b], writes=[bROW])
                for j in range(8):
                    mm(PS[1][:, 32 + 3 * j:35 + 3 * j], ROW[0:3, j * 128:(j + 1) * 128], CT[0:3, C_ID:C_ID + 3],
                       r=[bROW, bCT], w=[bPS[1]])
                evac(CCV[l][s][:], PS[1][:, 32:56].rearrange("p (j k) -> p j k", k=3), r=[bPS[1]], w=[bCAR[l][s]])
                for c0_ in (0, 1408):
                    P.dma("pool", ROW[0:2, 0:1408], st_fconv[l, b, :, c0_:c0_ + 1408], writes=[bROW])
                    for j in range(11):
                        jj_ = c0_ // 128 + j
                        mm(PS[1][:, 64 + 2 * jj_:66 + 2 * jj_], ROW[0:2, j * 128:(j + 1) * 128], CT[0:2, C_ID:C_ID + 2],
                           r=[bROW, bCT], w=[bPS[1]])
                evac(CFF[l][s][:], PS[1][:, 64:64 + 2 * NFT].rearrange("p (j k) -> p j k", k=2), r=[bPS[1]], w=[bCAR[l][s]])

        def rstd_of(xt, L, col):
            P.op("act", lambda e: e.activation(JK[0:L, :], xt, AF.Square, accum_out=SM[0:L, col:col + 1]),
                 reads=[bXr], writes=[bJK, bSM])
            P.op("act", lambda e: e.activation(SM[0:L, col:col + 1], SM[0:L, col:col + 1], AF.Ln, bias=1e-6, scale=1.0 / D),
                 reads=[bSM], writes=[bSM])
            P.op("act", lambda e: e.activation(SM[0:L, col:col + 1], SM[0:L, col:col + 1], AF.Exp, scale=-0.5),
                 reads=[bSM], writes=[bSM])

        HBx = YBb; bHBx = bYA
        SMN = sb("SMN", [128, 4]); bSMN = bufs("SMN", 4)

        def norm_gen(ci, off, L, gcol, k):
            hb, bhb = (HB, bHB) if k % 2 == 0 else (HBx, bHBx)
            pbk = 2 + (k % 2)
            P.op("act", lambda e: e.activation(JK[0:L, :], X[0:L, ci, :], AF.Square, accum_out=SMN[0:L, ci:ci + 1]),
                 reads=[bX[ci]], writes=[bJK, bSMN[ci]])
            yield
            P.op("act", lambda e: e.activation(SMN[0:L, ci:ci + 1], SMN[0:L, ci:ci + 1], AF.Ln, bias=1e-6, scale=1.0 / D),
                 reads=[bSMN[ci]], writes=[bSMN[ci]])
            yield
            P.op("act", lambda e: e.activation(SMN[0:L, ci:ci + 1], SMN[0:L, ci:ci + 1], AF.Exp, scale=-0.5),
                 reads=[bSMN[ci]], writes=[bSMN[ci]])
            yield
            P.op("dve", lambda e: e.tensor_scalar(hb[0:L, :], X[0:L, ci, :], SMN[0:L, ci:ci + 1], None, ALU.mult),
                 reads=[bX[ci], bSMN[ci]], writes=[bhb])
            yield
            for kc in range(8):
                P.op("pe", lambda e, kc=kc: e.transpose(PSB[pbk][:, kc * 128:kc * 128 + L], hb[0:L, kc * 128:(kc + 1) * 128], IDB[0:L, 0:L]),
                     reads=[bhb, bIDB], writes=[bPS[pbk]], inc=(kc == 7))
            yield
            P.op("dve", lambda e: e.tensor_tensor(
                HT[:, :, off:off + L], PSB[pbk].rearrange("p (k t) -> p k t", k=8)[:, :, 0:L],
                gcol.unsqueeze(2).to_broadcast([128, 8, L]), ALU.mult),
                reads=[bPS[pbk], bPRM], writes=[bHT])
            yield

        def norm_T(chunks, gcol):
            run_pipeline([norm_gen(ci, off, L, gcol, k) for k, (ci, off, L, seq) in enumerate(chunks)], 2)

        bXr = None

        def load_w_cols(wb, l, c0, ncols):
            ws, bw = wslot()
            v = ws[:, 0:8 * ncols].rearrange("p (k c) -> p k c", k=8)
            P.dma("sp", v, wb[l, :, c0:c0 + ncols].rearrange("(k p) c -> p k c", p=128), reads=[b_scr[(wb.tensor.name, l)]], writes=[bw])
            return v, bw

        def process_st(sti):
            nonlocal bXr
            mini = (sti == 0)
            if mini:
                chunks = [(0, 0, 16, 0), (1, 16, 16, 1), (2, 32, 16, 2)]
                N = 48
            else:
                t0 = (sti - 1) * NW
                chunks = [(c, c * 128, 128, 0) for c in range(NCH)]
                N = NW
            segs = []
            if mini:
                for (ci, off, L, seq) in chunks:
                    segs.append((seq, off, L, off + 3 * (ci + 1)))
            else:
                segs.append((0, 0, N, 3))
            pcol = {}
            for (seq, off, L, po) in segs:
                for (ci, coff, cL, cseq) in chunks:
                    if off <= coff < off + L:
                        pcol[ci] = po + (coff - off)
            last_of_seq = {}
            for (seq, off, L, po) in segs:
                last_of_seq[seq] = (seq > 0) or (sti == NST)
            for (ci, off, L, seq) in chunks:
                if mini:
                    src = meta if seq == 0 else xs[seq - 1]
                else:
                    src = xp[t0 + off:t0 + off + L, :]
                P.dma("pool", X[0:L, ci, :], src, writes=[bX[ci]])

            for l in range(2):
                pr = PRM[l]
                for i_, nm_ in enumerate(["a_ln_w", "a_ln_b", "b_hn_w", "a_w0"]):
                    P.dma("pool", LWs[i_][:], W[nm_][l].partition_broadcast(128), writes=[bLW])
                norm_T(chunks, pr["g1"][:])
                def gen_rwkv_in():
                    order = [12, 13, 4, 5, 6, 7, 0, 1, 2, 3, 8, 9, 10, 11]
                    wvs = {}
                    for j in order:
                        g, jj = j // 4, j % 4
                        if g not in wvs:
                            wvs[g] = load_w_cols(wb_in, l, g * 512, (4 if g < 3 else 2) * 128)
                        wv, bw = wvs[g]
                        pb = 3 + (j % 2)
                        for kc in range(8):
                            mm(PS[pb][:, 0:N], wv[:, kc, jj * 128:(jj + 1) * 128], HT[:, kc, 0:N], r=[bw, bHT], w=[bPS[pb]],
                               start=(kc == 0), stop=(kc == 7), inc=(kc == 7))
                        yield
                        for (seq, off, L, po) in segs:
                            evac(PT[:, j, po:po + L], PS[pb][:, off:off + L], r=[bPS[pb]], w=[bPT[j]] + bZT, eng="act")
                        yield
                        for (seq, off, L, po) in segs:
                            P.op("dve", lambda e, seq=seq, po=po, j=j: e.tensor_copy(PT[:, j, po - 1:po], CSH[l][seq][:, j:j + 1]),
                                 reads=[bCAR[l][seq]], writes=[bPT[j]])
                            P.op("dve", lambda e, seq=seq, po=po, L=L, j=j: e.tensor_copy(CSH[l][seq][:, j:j + 1], PT[:, j, po + L - 1:po + L]),
                                 reads=[bPT[j]], writes=[bCAR[l][seq]])
                            P.op("dve", lambda e, j=j, po=po, L=L: e.tensor_tensor(TMPA[:, 0:L], PT[:, j, po - 1:po + L - 1], PT[:, j, po:po + L], ALU.subtract),
                                 reads=[bPT[j]], writes=[bTA])
                            P.op("dve", lambda e, j=j, po=po, L=L: e.scalar_tensor_tensor(PT[:, j, po:po + L], TMPA[:, 0:L], pr["mu"][:, j:j + 1],
                                                                                         PT[:, j, po:po + L], ALU.mult, ALU.add),
                                 reads=[bTA, bPT[j], bPRM], writes=[bPT[j]])
                        yield
                    for (seq, off, L, po) in segs:
                        sl = slice(po, po + L)
                        ol = slice(off, off + L)
                        P.op("act", lambda e, sl=sl, ol=ol: e.activation(TW[0:64, ol], PT[0:64, 12, sl], AF.Tanh), reads=[bPT[12]], writes=[bTW])
                        P.op("dve", lambda e, sl=sl, ol=ol: e.tensor_copy(TW[64:128, ol], PT[64:128, 12, sl]), reads=[bPT[12]], writes=[bTW])
                        P.op("act", lambda e, sl=sl, ol=ol: e.activation(SG[:, ol], PT[:, 13, sl], AF.Sigmoid), reads=[bPT[13]], writes=[bSG])
                        yield
                        L2 = 2 * L
                        for p in range(2):
                            for jj in range(2):
                                j = 2 * p + jj
                                mm(PS[5 + p][:, jj * L:(jj + 1) * L], pr["w2a2"][64:128, j * 128:(j + 1) * 128], TW[64:128, ol], r=[bPRM, bTW], w=[bPS[5 + p]])
                                P.op("dve", lambda e, j=j, jj=jj, p=p: e.tensor_scalar(S[2 + p][:, jj * L:(jj + 1) * L], PT[:, 4 + j, sl], pr["kk"][:, j:j + 1], None, ALU.mult),
                                     reads=[bPT[4 + j], bPRM], writes=[bS[2 + p]])
                            yield
                            P.op("dve", lambda e, p=p: e.tensor_tensor(HB[:, p * L2:(p + 1) * L2], S[2 + p][:, 0:L2], S[2 + p][:, 0:L2], ALU.mult), reads=[bS[2 + p]], writes=[bHB])
                            for jj in range(2):
                                j = 2 * p + jj
                                P.op("act", lambda e, j=j, jj=jj, p=p: e.activation(S[p][:, jj * L:(jj + 1) * L], PS[5 + p][:, jj * L:(jj + 1) * L], AF.Sigmoid, bias=pr["a0"][:, j:j + 1]),
                                     reads=[bPS[5 + p], bPRM], writes=[bS[p]])
                            yield
                            pbk = 2 if p == 0 else 7
                            P.op("pe", lambda e, p=p, pbk=pbk: e.matmul(PS[pbk][:, 0:L2], BLKB[:, :], HB[:, p * L2:(p + 1) * L2], start=True, stop=True),
                                 reads=[bBLKB, bHB], writes=[bPS[pbk]])
                            yield
                        for p in range(2):
                            pbk = 2 if p == 0 else 7
                            P.op("act", lambda e, p=p, pbk=pbk: e.activation(JK[:, p * L2:(p + 1) * L2], PS[pbk][:, 0:L2], AF.Ln, bias=1e-12), reads=[bPS[pbk]], writes=[bJK])
                        P.op("act", lambda e: e.activation(JK[:, 0:2 * L2], JK[:, 0:2 * L2], AF.Exp, scale=-0.5), reads=[bJK], writes=[bJK])
                        yield
                        for p in range(2):
                            v3 = lambda t: t[:, 0:L2].rearrange("p (a t) -> p a t", a=2)
                            P.op("dve", lambda e, p=p: e.tensor_tensor(S[2 + p][:, 0:L2], S[2 + p][:, 0:L2], JK[:, p * L2:(p + 1) * L2], ALU.mult), reads=[bS[2 + p], bJK], writes=[bS[2 + p]])
                            yield
                            P.op("dve", lambda e, p=p: e.tensor_scalar(NKK[:, 2 * p:2 * p + 2, ol], v3(S[2 + p]), -1.0, None, ALU.mult), reads=[bS[2 + p]], writes=[bNB])
                            P.op("dve", lambda e, p=p: e.tensor_tensor(BBT[:, 2 * p:2 * p + 2, ol], v3(S[2 + p]), v3(S[p]), ALU.mult), reads=[bS[2 + p], bS[p]], writes=[bNB])
                            for jj in range(2):
                                j = 2 * p + jj
                                P.op("dve", lambda e, j=j, jj=jj, p=p: e.tensor_scalar(S[p][:, jj * L:(jj + 1) * L], S[p][:, jj * L:(jj + 1) * L], -1.0, pr["ka"][:, j:j + 1], ALU.add, ALU.mult),
                                     reads=[bS[p], bPRM], writes=[bS[p]])
                            yield
                            P.op("dve", lambda e, p=p: e.scalar_tensor_tensor(PT[:, 4 + 2 * p:6 + 2 * p, sl], v3(S[p]), 1.0, PT[:, 4 + 2 * p:6 + 2 * p, sl], ALU.add, ALU.mult),
                                 reads=[bS[p], bPT[4 + 2 * p], bPT[5 + 2 * p]], writes=[bPT[4 + 2 * p], bPT[5 + 2 * p]])
                            yield

                def gen_qk_in():
                    for g in range(2):
                        wv, bw = load_w_cols(wb_in, l, A_COLS + g * 512, 512)
                        for jj in range(4):
                            j = g * 4 + jj
                            pb = j % 2
                            U, bU, UA, bUA = U2[j % 2], bU2[j % 2], UA2[j % 2], bUA2[j % 2]
                            for kc in range(8):
                                mm(PS[pb][:, 0:N], wv[:, kc, jj * 128:(jj + 1) * 128], HT[:, kc, 0:N], r=[bw, bHT], w=[bPS[pb]],
                                   start=(kc == 0), stop=(kc == 7), inc=(kc == 7))
                            yield
                            for (seq, off, L, po) in segs:
                                evac(U[:, po:po + L], PS[pb][:, off:off + L], r=[bPS[pb]], w=[bU], eng="act")
                            yield
                            for (seq, off, L, po) in segs:
                                P.op("dve", lambda e, j=j, seq=seq, po=po: e.tensor_copy(U[:, po - 3:po], CCV[l][seq][:, j, :]),
                                     reads=[bCAR[l][seq]], writes=[bU])
                                P.op("dve", lambda e, j=j, seq=seq, po=po, L=L: e.tensor_copy(CCV[l][seq][:, j, :], U[:, po + L - 3:po + L]),
                                     reads=[bU], writes=[bCAR[l][seq]])
                            yield
                            for (seq, off, L, po) in segs:
                                P.op("act", lambda e, j=j, po=po, L=L, off=off: e.activation(UA[:, off:off + L], U[:, po - 3:po - 3 + L], AF.Identity,
                                                                                              bias=pr["cb"][:, j:j + 1], scale=pr["cw"][:, 0, j:j + 1]),
                                     reads=[bU, bPRM], writes=[bUA])
                            yield
                            for tpi in (1, 2, 3):
                                for (seq, off, L, po) in segs:
                                    P.op(POOL_ENG, lambda e, j=j, po=po, L=L, off=off, tpi=tpi: e.scalar_tensor_tensor(
                                        UA[:, off:off + L], U[:, po - 3 + tpi:po - 3 + tpi + L], pr["cw"][:, tpi, j:j + 1], UA[:, off:off + L], ALU.mult, ALU.add),
                                        reads=[bU, bPRM, bUA], writes=[bUA])
                                yield
                            if j >= 4:
                                P.op("act", lambda e, j=j, N=N: e.activation(UA[:, 0:N], UA[:, 0:N], AF.Silu), reads=[bUA], writes=[bUA])
                                yield
                                P.op("dve", lambda e, j=j, N=N: e.tensor_scalar(QKT[:, j, 0:N], UA[:, 0:N], 128.0 ** -0.5, None, ALU.mult),
                                     reads=[bUA], writes=[bQK])
                            else:
                                P.op("act", lambda e, j=j, N=N: e.activation(QKT[:, j, 0:N], UA[:, 0:N], AF.Silu), reads=[bUA], writes=[bQK])
                            yield

                run_threads([gen_rwkv_in(), gen_qk_in()])
                for (seq, off, L, po) in segs:
                    if last_of_seq[seq]:
                        tl = off + L - 1
                        for g in range(4):
                            ncol = 512 if g < 3 else 256
                            wv, bw = load_w_cols(wb_in, l, g * 512, ncol)
                            for kc in range(8):
                                mm(PS[5][0:1, 0:ncol], HT[:, kc, tl:tl + 1], wv[:, kc, 0:ncol], r=[bw, bHT], w=[bPS[5]],
                                   start=(kc == 0), stop=(kc == 7), inc=(kc == 7))
                            evac(ROW[0:1, (g % 3) * 512:(g % 3) * 512 + ncol], PS[5][0:1, 0:ncol], r=[bPS[5]], w=[bROW])
                            P.dma("pool", o_shift[l, seq:seq + 1, g * 512:g * 512 + ncol], ROW[0:1, (g % 3) * 512:(g % 3) * 512 + ncol], reads=[bROW])
                for (seq, off, L, po) in segs:
                    if last_of_seq[seq]:
                        tl = off + L - 3
                        for g in range(2):
                            wv, bw = load_w_cols(wb_in, l, A_COLS + g * 512, 512)
                            for kc in range(8):
                                mm(PS[5][0:3, 0:512], HT[:, kc, tl:tl + 3], wv[:, kc, 0:512], r=[bw, bHT], w=[bPS[5]],
                                   start=(kc == 0), stop=(kc == 7), inc=(kc == 7))
                            evac(ROW[0:3, g * 512:(g + 1) * 512], PS[5][0:3, 0:512], r=[bPS[5]], w=[bROW])
                        P.dma("pool", o_conv[l, seq], ROW[0:3, 0:1024], reads=[bROW])

                wv_vo = []
                for g in range(2):
                    wv_vo.append(load_w_cols(wb_in, l, A_COLS + 1024 + g * 512, 512))
                bw_g = bWG
                wv_gt = WG[:, 0:64].rearrange("p (k c) -> p k c", k=8)
                P.dma("sp", wv_gt, wb_in[l, :, D_IN - 8:D_IN].rearrange("(k p) c -> p k c", p=128), reads=[b_scr[(wb_in.tensor.name, l)]], writes=[bw_g])

                nck_ = len(chunks)
                Rdone = [False] * nck_
                Mdone = [False] * nck_
                Yiss = [False] * nck_
                HSIG.clear()
                Rg = Mg = None
                rc = mc = 0
                rcur = mcur = -1
                while not all(Yiss):
                    if Rg is None and rc < nck_ and (rc == 0 or Yiss[rc - 1]):
                        (ci, off, L, seq) = chunks[rc]
                        Rg = rwkv_gen(l, pr, ci, off, L, seq, pcol[ci])
                        rcur = rc
                        rc += 1
                    if Mg is None and mc < nck_ and (mc == 0 or HSIG.get(chunks[mc - 1][0], False)):
                        (ci, off, L, seq) = chunks[mc]
                        Mg = mlstm_gen(l, pr, ci, off, L, seq, wv_vo, (wv_gt, bw_g), guard=(lambda k_=mc: k_ == 0 or Yiss[k_ - 1]))
                        mcur = mc
                        mc += 1
                    if Rg is not None:
                        mdone["m"] = Mdone[rcur] and (Mg is None)
                        try:
                            next(Rg)
                        except StopIteration:
                            Rdone[rcur] = True
                            Rg = None
                    if Mg is not None:
                        try:
                            next(Mg)
                        except StopIteration:
                            Mdone[mcur] = True
                            Mg = None
                    for k_ in range(nck_):
                        if Rdone[k_] and Mdone[k_] and not Yiss[k_]:
                            (ci, off, L, seq) = chunks[k_]
                            dump("y", l, ci, YBb[0:L, :], BF16, [bYA])
                            for kc in range(8):
                                P.op("pe", lambda e, kc=kc, L=L: e.transpose(PSB[2][:, kc * 128:kc * 128 + L], YBb[0:L, kc * 128:(kc + 1) * 128], IDB[0:L, 0:L]),
                                     reads=[bYA, bIDB], writes=[bPS[2]], inc=(kc == 7))
                            evac(YT[:, :, off:off + L], PSB[2].rearrange("p (k t) -> p k t", k=8)[:, :, 0:L], r=[bPS[2]], w=[bYT])
                            Yiss[k_] = True
                for (seq, off, L, po) in segs:
                    if last_of_seq[seq]:
                        for h in range(8):
                            mm(PS[0][0:64, h * 64:(h + 1) * 64], Hs[l][seq][:, h * 64:(h + 1) * 64], CT[0:64, C_ID:C_ID + 64],
                               r=[bH[l][seq], bCT], w=[bPS[0]])
                        evac(ZS[:, :], PS[0][0:64, :], r=[bPS[0]], w=[bZS])
                        P.dma("pool", o_wkv[l, seq].rearrange("h v k -> v h k"), ZS[:, :].rearrange("p (h k) -> p h k", h=8), reads=[bZS])
                        P.dma("pool", o_C[l, seq].rearrange("h d e -> d h e"), CE[l][seq][:, :, 0:128], reads=[bCE[l][seq]])
                        P.dma("pool", o_n[l, seq].rearrange("h (d o) -> d h o", o=1), CE[l][seq][:, :, 128:129], reads=[bCE[l][seq]],
                              allow_slow_non_contiguous=True)
                        P.dma("pool", o_m[l, seq:seq + 1, :], M0[l][seq][0:1, :], reads=[bM0[l][seq]])

                wo = []
                for g in range(2):
                    ws, bw = wslot()
                    v = ws[:, 0:4096].rearrange("p (k c) -> p k c", k=4)
                    P.dma("sp", v, wb_out[l, g * 512:(g + 1) * 512, :].rearrange("(k p) c -> p k c", p=128), reads=[b_scr[(wb_out.tensor.name, l)]], writes=[bw])
                    wo.append((v, bw))
                for (ci, off, L, seq) in chunks:
                    for hf in range(2):
                        pb = 3 + hf
                        for kc in range(8):
                            v, bw = wo[kc // 4]
                            mm(PS[pb][0:L, :], YT[:, kc, off:off + L], v[:, kc % 4, hf * 512:(hf + 1) * 512], r=[bYT, bw], w=[bPS[pb]],
                               start=(kc == 0), stop=(kc == 7), inc=(kc == 7))
                        P.op("dve", lambda e, ci=ci, L=L, hf=hf, pb=pb: e.tensor_tensor(X[0:L, ci, hf * 512:(hf + 1) * 512], X[0:L, ci, hf * 512:(hf + 1) * 512],
                                                                                        PS[pb][0:L, :], ALU.add),
                             reads=[bPS[pb], bX[ci]], writes=[bX[ci]])

                for (ci, off, L, seq) in chunks:
                    dump("xmix", l, ci, X[0:L, ci, :], F32, [bX[ci]])
                norm_T(chunks, pr["g2"][:])
                wup = {}

                def ffn_gen(f):
                    if f % 4 == 0:
                        ncw = min(4, NFT - f) * 128
                        wup[f // 4] = (load_w_cols(wb_up, l, f * 128, ncw), load_w_cols(wb_up, l, D_FF + f * 128, ncw))
                    (wa, bwa), (wg, bwg) = wup[f // 4]
                    fo = (f % 4) * 128
                    U, bU, UA, bUA = U2[f % 3], bU2[f % 3], UA2[f % 3], bUA2[f % 3]
                    pa_i, pg_i = (f % 3), 3 + (f % 3)
                    for kc in range(8):
                        mm(PS[pa_i][:, 0:N], wa[:, kc, fo:fo + 128], HT[:, kc, 0:N], r=[bwa, bHT], w=[bPS[pa_i]], start=(kc == 0), stop=(kc == 7), inc=(kc == 7))
                    for kc in range(8):
                        mm(PS[pg_i][:, 0:N], wg[:, kc, fo:fo + 128], HT[:, kc, 0:N], r=[bwg, bHT], w=[bPS[pg_i]], start=(kc == 0), stop=(kc == 7), inc=(kc == 7))
                    yield
                    for (seq, off, L, po) in segs:
                        P.op("act", lambda e, po=po, off=off, L=L: e.copy(U[:, po:po + L], PS[pa_i][:, off:off + L]), reads=[bPS[pa_i]], writes=[bU])
                    yield
                    for (seq, off, L, po) in segs:
                        P.op("dve", lambda e, f=f, seq=seq, po=po: e.tensor_copy(U[:, po - 2:po], CFF[l][seq][:, f, :]), reads=[bCAR[l][seq]], writes=[bU])
                        P.op("dve", lambda e, f=f, seq=seq, po=po, L=L: e.tensor_copy(CFF[l][seq][:, f, :], U[:, po + L - 2:po + L]), reads=[bU], writes=[bCAR[l][seq]])
                    yield
                    for (seq, off, L, po) in segs:
                        P.op("act", lambda e, f=f, po=po, L=L, off=off: e.activation(UA[:, off:off + L], U[:, po - 2:po - 2 + L], AF.Identity,
                                                                                      bias=pr["fb"][:, f:f + 1], scale=pr["fw"][:, 0, f:f + 1]), reads=[bU, bPRM], writes=[bUA])
                    yield
                    for tpi in (1, 2):
                        for (seq, off, L, po) in segs:
                            P.op(POOL_ENG, lambda e, f=f, po=po, L=L, off=off, tpi=tpi: e.scalar_tensor_tensor(
                                UA[:, off:off + L], U[:, po - 2 + tpi:po - 2 + tpi + L], pr["fw"][:, tpi, f:f + 1], UA[:, off:off + L], ALU.mult, ALU.add),
                                reads=[bU, bPRM, bUA], writes=[bUA])
                        yield
                    P.op("act", lambda e, N=N: e.activation(UA[:, 0:N], UA[:, 0:N], AF.Silu), reads=[bUA], writes=[bUA])
                    yield
                    P.op("dve", lambda e, f=f, N=N: e.tensor_tensor(ZT[:, f, 0:N], UA[:, 0:N], PS[pg_i][:, 0:N], ALU.mult), reads=[bUA, bPS[pg_i]], writes=[bZT[f]] + bPT)
                    yield

                run_pipeline([ffn_gen(f) for f in range(NFT)], 3)
                for (seq, off, L, po) in segs:
                    if last_of_seq[seq]:
                        tl = off + L - 2
                        for g in range(6):
                            ncol = 512 if g < 5 else 256
                            wv, bw = load_w_cols(wb_up, l, g * 512, ncol)
                            for kc in range(8):
                                mm(PS[5][0:2, 0:ncol], HT[:, kc, tl:tl + 2], wv[:, kc, 0:ncol], r=[bw, bHT], w=[bPS[5]],
                                   start=(kc == 0), stop=(kc == 7), inc=(kc == 7))
                            evac(ROW[0:2, (g % 3) * 512:(g % 3) * 512 + ncol], PS[5][0:2, 0:ncol], r=[bPS[5]], w=[bROW])
                            P.dma("pool", o_fconv[l, seq, :, g * 512:g * 512 + ncol], ROW[0:2, (g % 3) * 512:(g % 3) * 512 + ncol], reads=[bROW])
                nck = len(chunks)
                pbase = 8 - 2 * nck
                for g in range(6):
                    nf = 4 if g < 5 else 2
                    ws, bw = wslot()
                    v = ws[:, 0:nf * 1024].rearrange("p (k c) -> p k c", k=nf)
                    P.dma("sp", v, wb_down[l, g * 512:g * 512 + nf * 128, :].rearrange("(k p) c -> p k c", p=128),
                          reads=[b_scr[(wb_down.tensor.name, l)]], writes=[bw])
                    for (ci, off, L, seq) in chunks:
                        for hf in range(2):
                            pb = pbase + 2 * ci + hf
                            for ff in range(nf):
                                f = g * 4 + ff
                                mm(PS[pb][0:L, :], ZT[:, f, off:off + L], v[:, ff, hf * 512:(hf + 1) * 512], r=[bZT[f], bw], w=[bPS[pb]],
                                   start=(f == 0), stop=(f == NFT - 1), inc=(ff == nf - 1))
                for (ci, off, L, seq) in chunks:
                    for hf in range(2):
                        pb = pbase + 2 * ci + hf
                        P.op("dve", lambda e, ci=ci, L=L, hf=hf, pb=pb: e.tensor_tensor(X[0:L, ci, hf * 512:(hf + 1) * 512], X[0:L, ci, hf * 512:(hf + 1) * 512],
                                                                                        PS[pb][0:L, :], ALU.add),
                             reads=[bPS[pb], bX[ci]], writes=[bX[ci]])
            for (ci, off, L, seq) in chunks:
                if mini and seq == 0:
                    continue
                bXr = bX[ci]
                rstd_of(X[0:L, ci, :], L, 1)
                P.op("dve", lambda e, ci=ci, L=L: e.scalar_tensor_tensor(JK[0:L, :], X[0:L, ci, :], SM[0:L, 1:2], GF[0:L, :], ALU.mult, ALU.mult),
                     reads=[bX[ci], bSM, bPRM], writes=[bJK])
                dst = y_s[seq - 1] if mini else y_p[t0 + off:t0 + off + L, :]
                P.dma("pool", dst, JK[0:L, :], reads=[bJK])

        BLKB = sb("BLKB", [128, 128], BF16); bBLKB = Buf("BLKB")
        P.op("dve", lambda e: e.tensor_copy(BLKB[:], CT[:, C_BLK:C_BLK + 128]), reads=[bCT], writes=[bBLKB])
        M2 = sb("M2", [128, 2, 128]); bM2 = Buf("M2")
        P.op("dve", lambda e: e.tensor_copy(M2[:, 0, :], CT[:, C_TRIS:C_TRIS + 128]), reads=[bCT], writes=[bM2])
        P.op("dve", lambda e: e.tensor_copy(M2[:, 1, :], CT[:, C_TRII:C_TRII + 128]), reads=[bCT], writes=[bM2])

        def run_pipeline(gens, depth):
            gens = list(gens)
            live = []
            while gens or live:
                while gens and len(live) < depth:
                    live.append(gens.pop(0))
                nl = []
                for g_ in live:
                    try:
                        next(g_)
                        nl.append(g_)
                    except StopIteration:
                        pass
                live = nl

        def run_threads(gens):
            gens = list(gens)
            if SEQ_THREADS:
                for g_ in gens:
                    for _ in g_:
                        pass
                return
            while gens:
                nxt = []
                for g_ in gens:
                    try:
                        next(g_)
                        nxt.append(g_)
                    except StopIteration:
                        pass
                gens = nxt

        NSLOT = 4
        SABs = [sb(f"SABs{i}", [128, 2, 2, 128], BF16) for i in range(NSLOT)]; bSABs = bufs("SABs", NSLOT)
        SN2s = [[sb(f"SN2s{i}_{k}", [128, 2, 128], BF16) for k in range(2)] for i in range(NSLOT)]
        bSN2s = [bufs(f"SN2s{i}_", 2) for i in range(NSLOT)]
        XBs = [[sb(f"XBs{i}_{k}", [128, 128], BF16) for k in range(2)] for i in range(NSLOT)]
        bXBs = [bufs(f"XBs{i}_", 2) for i in range(NSLOT)]
        mdone = {"m": True}
        HSIG = {}
        G8R = sb("G8R", [128, 16]); bG8R = Buf("G8R")
        SQ = [sb(f"SQ{i}", [128, 512]) for i in range(3)]; bSQ = bufs("SQ", 3)
        JKA = LDT; bJKA = bLDT
        JKB = SQ[2]; bJKB = bSQ[2]

        def psv(bank, L):
            base = bank[0:L, 0:1]
            return bass.AP(base.tensor, base.offset, [list(base.ap[0]), [256, 2], [L, 2], [1, L]])

        def rwkv_head_gen(l, pr, h, L, slot, nlev):
            A, bA = PS[2 * slot], bPS[2 * slot]
            B, bB = PS[2 * slot + 1], bPS[2 * slot + 1]
            SAB, bSAB = SABs[slot], bSABs[slot]
            SN2, bSN2 = SN2s[slot], bSN2s[slot]
            XB, bXB = XBs[slot], bXBs[slot]
            j, hh = h // 2, h % 2
            prt = slice(hh * 64, hh * 64 + 64)
            cr = slice(h * 64, h * 64 + 64)
            arT = AR[prt, j, :, 0:L]
            mm(A[0:L, 0:2 * L].rearrange("p (a t) -> p a t", a=2), BTt[prt, j, 0:L], arT, r=[bBK, bAR], w=[bA], inc=False)
            mm(A[0:L, 256:256 + 2 * L].rearrange("p (a t) -> p a t", a=2), KTt[prt, j, 0:L], arT, r=[bBK, bAR], w=[bA])
            mm(B[0:L, 0:L], AR[prt, j, 0, 0:L], BTt[prt, j, 0:L], r=[bBK, bAR], w=[bB])
            yield
            P.op("dve", lambda e: e.tensor_tensor(SAB[0:L, :, :, 0:L], psv(A, L), M2[0:L, :, 0:L].unsqueeze(1).to_broadcast([L, 2, 2, L]), ALU.mult),
                 reads=[bA, bM2], writes=[bSAB])
            P.op("dve", lambda e: e.tensor_tensor(SN2[0][0:L, 1, 0:L], B[0:L, 0:L], cs(C_LOS, L), ALU.mult), reads=[bB, bCT], writes=[bSN2[0]])
            yield
            mm(B[0:L, 128:192], SAB[0:L, 1, 0, 0:L], TOK[0:L, 3, cr], r=[bSAB, bTOK], w=[bB])
            P.op("act", lambda e: e.copy(XB[0][0:L, 64:128], TOK[0:L, 0, cr]), reads=[bTOK], writes=[bXB[0]])
            yield
            P.op("act", lambda e: e.copy(XB[0][0:L, 0:64], B[0:L, 128:192]), reads=[bB], writes=[bXB[0]])
            yield
            cur_ = 0
            nt_ap, bnt = SAB[0:L, 0, 0, 0:L], bSAB
            n_ap, bn = SN2[0][0:L, 1, 0:L], bSN2[0]
            for lev in range(nlev):
                nx = 1 - cur_
                if lev < nlev - 1:
                    mm(A[0:L, 0:L], n_ap, nt_ap, r=[bn, bnt], w=[bA], inc=False)
                    mm(A[0:L, 128:128 + L], nt_ap, n_ap, r=[bn, bnt], w=[bA])
                mm(B[0:L, 256:384], nt_ap, XB[cur_][0:L, :], r=[bnt, bXB[cur_]], w=[bB])
                yield
                P.op("dve", lambda e, cur_=cur_, nx=nx: e.tensor_tensor(XB[nx][0:L, :], XB[cur_][0:L, :], B[0:L, 256:384], ALU.add),
                     reads=[bB, bXB[cur_]], writes=[bXB[nx]])
                if lev < nlev - 1:
                    evac(SN2[nx][0:L, :, 0:L], A[0:L, 0:256].rearrange("p (a t) -> p a t", a=2)[:, :, 0:L], r=[bA], w=[bSN2[nx]], eng="act")
                    nt_ap, bnt = SN2[nx][0:L, 0, 0:L], bSN2[nx]
                    n_ap, bn = SN2[nx][0:L, 1, 0:L], bSN2[nx]
                yield
                cur_ = nx
            XF, bXF = XB[cur_], bXB[cur_]
            mm(A[0:L, 256:320], SAB[0:L, 0, 1, 0:L], XF[0:L, 0:64], r=[bSAB, bXF], w=[bA], start=True, stop=False, inc=False)
            mm(A[0:L, 256:320], SAB[0:L, 1, 1, 0:L], TOK[0:L, 3, cr], r=[bSAB, bTOK], w=[bA], start=False, stop=True, inc=False)
            mm(A[0:64, 320:320 + L], XF[0:L, 64:128], SAB[0:L, 0, 1, 0:L], r=[bXF, bSAB], w=[bA], start=True, stop=False, inc=False)
            mm(A[0:64, 320:320 + L], IDB[prt, hh * 64:hh * 64 + 64], AR[prt, j, 1, 0:L], r=[bIDB, bAR], w=[bA], start=False, stop=True)
            mm(B[0:64, 0:64], CT[prt, C_ID + hh * 64:C_ID + hh * 64 + 64], DG[prt, j, :], r=[bCT, bDG], w=[bB], start=True, stop=False, inc=False)
            mm(B[0:64, 0:64], XF[0:L, 64:128], TOK[0:L, 1, cr], r=[bXF, bTOK], w=[bB], start=False, stop=True, inc=False)
            mm(B[0:64, 64:128], TOK[0:L, 1, cr], XF[0:L, 0:64], r=[bTOK, bXF], w=[bB], start=True, stop=False, inc=False)
            mm(B[0:64, 64:128], TOK[0:L, 2, cr], TOK[0:L, 3, cr], r=[bTOK], w=[bB], start=False, stop=True)
            yield
            evac(Y0[0:L, cr], A[0:L, 256:320], r=[bA], w=[bY0], eng="act")
            evac(MT[:, h, :], B[0:64, 0:64], r=[bB], w=[bMT], eng="dve")
            evac(QT[:, h, 0:L], A[0:64, 320:320 + L], r=[bA], w=[bQT], eng="act")
            evac(ZS[:, cr], B[0:64, 64:128], r=[bB], w=[bZS], eng="dve")
            yield

        def rwkv_gen(l, pr, ci, off, L, seq, pc):
            nlev = int(np.log2(L))
            sl = slice(pc, pc + L)
            ol = slice(off, off + L)
            R = PS[0:4]
            bR = bPS[0:4]
            mm(R[0][0:L, :], TW[0:64, ol], pr["w2a2"][0:64, :], r=[bTW, bPRM], w=[bR[0]])
            yield
            P.op("dve", lambda e: e.tensor_tensor(LDT[0:L, :], R[0][0:L, :], pr["w0"][0:L, :], ALU.add), reads=[bR[0], bLW], writes=[bLDT])
            yield
            P.op("act", lambda e: e.activation(LDT[0:L, :], LDT[0:L, :], AF.Exp, scale=-1.0), reads=[bLDT], writes=[bLDT])
            yield
            P.op("act", lambda e: e.activation(LDT[0:L, :], LDT[0:L, :], AF.Ln, bias=1.0), reads=[bLDT], writes=[bLDT])
            yield
            P.op("act", lambda e: e.activation(LDT[0:L, :], LDT[0:L, :], AF.Exp, scale=-1.0), reads=[bLDT], writes=[bLDT])
            yield
            for j in range(4):
                mm(R[1][:, j * 128:j * 128 + L], LDT[0:L, j * 128:(j + 1) * 128], cs(C_TRIIW, L), r=[bLDT, bCT], w=[bR[1]], inc=False)
                mm(R[2][:, j * 128:j * 128 + L], LDT[0:L, j * 128:(j + 1) * 128], cs(C_TRISW, L), r=[bLDT, bCT], w=[bR[2]], inc=(j == 3))
            yield
            p0 = R[1][:].rearrange("p (j t) -> p j t", j=4)[:, :, 0:L]
            p1 = R[2][:].rearrange("p (j t) -> p j t", j=4)[:, :, 0:L]
            s3 = lambda i: S[i][:].rearrange("p (j t) -> p j t", j=4)[:, :, 0:L]
            P.op("dve", lambda e: e.tensor_copy(SM[:, 4:8].unsqueeze(2), p0[:, :, L - 1:L]), reads=[bR[1]], writes=[bSM])
            P.op("act", lambda e: e.activation(s3(1), p1, AF.Exp), reads=[bR[2]], writes=[bS[1]])
            P.op("act", lambda e: e.activation(s3(0), p0, AF.Exp), reads=[bR[1]], writes=[bS[0]])
            yield
            P.op("dve", lambda e: e.tensor_tensor(AR[:, :, 0, 0:L], NKK[:, :, ol], s3(1), ALU.mult), reads=[bNB, bS[1]], writes=[bAR])
            P.op("act", lambda e: e.activation(s3(2), p0, AF.Exp, scale=-1.0), reads=[bR[1]], writes=[bS[2]])
            yield
            P.op("dve", lambda e: e.tensor_tensor(AR[:, :, 1, 0:L], PT[:, 0:4, sl], s3(0), ALU.mult), reads=bPT[0:4] + [bS[0]], writes=[bAR])
            for j in range(4):
                P.op("act", lambda e, j=j: e.activation(S[3][:, j * 128:j * 128 + L], R[1][:, j * 128:j * 128 + L], AF.Exp, bias=SM[:, 4 + j:5 + j], scale=-1.0),
                     reads=[bR[1], bSM], writes=[bS[3]])
            yield
            P.op("dve", lambda e: e.tensor_tensor(BTt[:, :, 0:L], BBT[:, :, ol], s3(2), ALU.mult), reads=[bNB, bS[2]], writes=[bBK])
            P.op("dve", lambda e: e.tensor_tensor(KTt[:, :, 0:L], PT[:, 4:8, sl], s3(2), ALU.mult), reads=bPT[4:8] + [bS[2]], writes=[bBK])
            P.op("act", lambda e: e.copy(VBT[:, :, 0:L], PT[:, 8:12, sl]), reads=bPT[8:12], writes=[bBKH])
            yield
            P.op("dve", lambda e: e.tensor_tensor(BHT[:, :, 0:L], BBT[:, :, ol], s3(3), ALU.mult), reads=[bNB, bS[3]], writes=[bBKH])
            P.op("dve", lambda e: e.tensor_tensor(KHT[:, :, 0:L], PT[:, 4:8, sl], s3(3), ALU.mult), reads=bPT[4:8] + [bS[3]], writes=[bBKH])
            P.op("dve", lambda e: e.tensor_tensor(DG[:], CT[:, C_ID2:C_ID2 + 64].unsqueeze(1).to_broadcast([128, 4, 64]),
                                                  s3(0)[:, :, L - 1:L].to_broadcast([128, 4, 64]), ALU.mult), reads=[bCT, bS[0]], writes=[bDG])
            yield
            if RW_STOP == 1:
                return
            for q, src in enumerate([AR, BHT, KHT, VBT]):
                bank = 0 if q < 2 else 3
                for j in range(4):
                    s_ap = (src[:, j, 0, 0:L] if q == 0 else src[:, j, 0:L])
                    P.op("pe", lambda e, s_ap=s_ap, q=q, j=j, bank=bank: e.transpose(PSB[bank][0:L, (q % 2) * 512 + j * 128:(q % 2) * 512 + (j + 1) * 128], s_ap, IDB[:, :]),
                         reads=[bAR if q == 0 else bBKH, bIDB], writes=[bR[bank]], inc=(j == 3))
            yield
            evac(TOK[0:L, 0:2, :], PSB[0][0:L, :].rearrange("p (q c) -> p q c", q=2), r=[bR[0]], w=[bTOK])
            evac(TOK[0:L, 2:4, :], PSB[3][0:L, :].rearrange("p (q c) -> p q c", q=2), r=[bR[3]], w=[bTOK])
            if RW_STOP == 2:
                return
            for j in range(4):
                P.op("dve", lambda e, j=j: e.scalar_tensor_tensor(TMPA[:, 0:L], PT[:, j, sl], pr["rk"][:, j:j + 1], PT[:, 4 + j, sl], ALU.mult, ALU.mult),
                     reads=[bPT[j], bPT[4 + j], bPRM], writes=[bTA])
                mm(R[2][0:L, 500 + 2 * j:502 + 2 * j], TMPA[:, 0:L], CT[:, C_HSEL:C_HSEL + 2], r=[bTA, bCT], w=[bR[2]])
                yield
            P.op("dve", lambda e: e.tensor_copy(SM[0:L, 8:16], R[2][0:L, 500:508]), reads=[bR[2]], writes=[bSM])
            yield
            if RW_STOP == 3:
                return
            todo = list(range(8))
            free = [0, 1]
            extra = False
            live = []
            while todo or live:
                if (not extra) and mdone["m"]:
                    extra = True
                    free += [2, 3]
                while todo and free:
                    sl_ = free.pop(0)
                    live.append((rwkv_head_gen(l, pr, todo.pop(0), L, sl_, nlev), sl_))
                nl = []
                for g_, sl_ in live:
                    try:
                        next(g_)
                        nl.append((g_, sl_))
                    except StopIteration:
                        free.append(sl_)
                live = nl
                yield
            HSIG[ci] = True
            Hc = Hs[l][seq]
            bHc = bH[l][seq]
            P.op("act", lambda e: e.copy(HBF[:, :], Hc[:, :]), reads=[bHc], writes=[bHBF])
            for h in range(8):
                cr = slice(h * 64, h * 64 + 64)
                mm(R[1][0:64, cr], MT[:, h, :], Hc[:, cr], r=[bMT, bHc], w=[bR[1]], inc=(h == 7))
            mm(R[2][0:L, :], SG[:, ol], pr["g2w"][:, :], r=[bSG, bPRM], w=[bR[2]])
            yield
            for h in range(8):
                cr = slice(h * 64, h * 64 + 64)
                mm(R[0][0:L, cr], QT[:, h, 0:L], HBF[:, cr], r=[bQT, bHBF], w=[bR[0]], inc=(h == 7))
            yield
            P.op("dve", lambda e: e.tensor_tensor(Hc[:, :], ZS[:, :], R[1][0:64, :], ALU.add), reads=[bZS, bR[1]], writes=[bHc])
            P.op("dve", lambda e: e.tensor_tensor(Y0[0:L, :], Y0[0:L, :], R[0][0:L, :], ALU.add), reads=[bY0, bR[0]], writes=[bY0])
            dump("y0pre", l, ci, Y0[0:L, :], F32, [bY0])
            dump("hnew", l, ci, Hc[:, :], F32, [bHc])
            yield
            y3 = Y0[0:L, :].rearrange("p (h c) -> p h c", h=8)
            st8 = lambda a, b_: G8R[0:L, a:b_]
            P.op("dve", lambda e: e.tensor_reduce(st8(0, 8), y3, AX.X, ALU.add), reads=[bY0], writes=[bG8R])
            yield
            P.op("dve", lambda e: e.tensor_scalar(st8(0, 8), st8(0, 8), 1.0 / 64, None, ALU.mult), reads=[bG8R], writes=[bG8R])
            yield
            P.op("dve", lambda e: e.tensor_tensor(y3, y3, st8(0, 8).unsqueeze(2).to_broadcast([L, 8, 64]), ALU.subtract), reads=[bY0, bG8R], writes=[bY0])
            yield
            j3 = JKA[0:L, 0:512].rearrange("p (h c) -> p h c", h=8)
            P.op("dve", lambda e: e.tensor_tensor(j3, y3, y3, ALU.mult), reads=[bY0], writes=[bJKA])
            yield
            P.op("dve", lambda e: e.tensor_reduce(st8(8, 16), j3, AX.X, ALU.add), reads=[bJKA], writes=[bG8R])
            yield
            P.op("act", lambda e: e.activation(st8(8, 16), st8(8, 16), AF.Ln, bias=64e-5, scale=1.0 / 64), reads=[bG8R], writes=[bG8R])
            yield
            P.op("act", lambda e: e.activation(st8(8, 16), st8(8, 16), AF.Exp, scale=-0.5), reads=[bG8R], writes=[bG8R])
            P.op("dve", lambda e: e.tensor_tensor(j3, TOK[0:L, 3, :].rearrange("p (h c) -> p h c", h=8), SM[0:L, 8:16].unsqueeze(2).to_broadcast([L, 8, 64]), ALU.mult),
                 reads=[bTOK, bSM], writes=[bJKA])
            yield
            P.op("dve", lambda e: e.tensor_tensor(y3, y3, st8(8, 16).unsqueeze(2).to_broadcast([L, 8, 64]), ALU.mult), reads=[bY0, bG8R], writes=[bY0])
            yield
            P.op("dve", lambda e: e.tensor_tensor(Y0[0:L, :], Y0[0:L, :], pr["lw"][0:L, :], ALU.mult), reads=[bY0, bLW], writes=[bY0])
            yield
            P.op("dve", lambda e: e.tensor_tensor(Y0[0:L, :], Y0[0:L, :], pr["lb"][0:L, :], ALU.add), reads=[bY0, bLW], writes=[bY0])
            yield
            P.op("dve", lambda e: e.tensor_tensor(Y0[0:L, :], Y0[0:L, :], JKA[0:L, 0:512], ALU.add), reads=[bY0, bJKA], writes=[bY0])
            yield
            P.op("dve", lambda e: e.tensor_tensor(YBb[0:L, 0:512], Y0[0:L, :], R[2][0:L, :], ALU.mult), reads=[bY0, bR[2]], writes=[bYA])
            yield

        def mlstm_gen(l, pr, ci, off, L, seq, wv_vo, wgt, guard=lambda: True):
            ol = slice(off, off + L)
            M = PS[4:8]
            bM = bPS[4:8]
            for g in range(2):
                wv, bw = wv_vo[g]
                for kc in range(8):
                    mm(M[g][0:L, :], HT[:, kc, ol], wv[:, kc, :], r=[bHT, bw], w=[bM[g]], start=(kc == 0), stop=(kc == 7), inc=(kc == 7))
                yield
            wg, bwg = wgt
            for kc in range(8):
                mm(M[2][0:L, 0:8], HT[:, kc, ol], wg[:, kc, :], r=[bHT, bwg], w=[bM[2]], start=(kc == 0), stop=(kc == 7), inc=(kc == 7))
            yield
            P.op("act", lambda e: e.copy(VE[0:L, :, 0:128], M[0][0:L, :].rearrange("p (h e) -> p h e", h=4)), reads=[bM[0]], writes=[bVE])
            P.op("pool", lambda e: e.memset(VE[0:L, :, 128:129], 1.0), writes=[bVE])
            g8 = lambda a, b_: G8[0:L, a:b_]
            P.op("dve", lambda e: e.tensor_tensor(g8(16, 24), M[2][0:L, 0:8], pr["ifb"][0:L, :], ALU.add), reads=[bM[2], bPRM], writes=[bG8])
            yield
            P.op("act", lambda e: e.activation(g8(20, 24), g8(20, 24), AF.Exp, scale=-1.0), reads=[bG8], writes=[bG8])
            yield
            P.op("act", lambda e: e.activation(g8(20, 24), g8(20, 24), AF.Ln, bias=1.0), reads=[bG8], writes=[bG8])
            yield
            P.op("dve", lambda e: e.tensor_scalar(g8(20, 24), g8(20, 24), -1.0, None, ALU.mult), reads=[bG8], writes=[bG8])
            P.op("act", lambda e: e.activation(VO[0:L, 0:512], M[1][0:L, :], AF.Exp, scale=-1.0), reads=[bM[1]], writes=[bVO])
            yield
            P.op("act", lambda e: e.activation(VO[0:L, 0:512], VO[0:L, 0:512], AF.Ln, bias=1.0), reads=[bVO], writes=[bVO])
            yield
            P.op("act", lambda e: e.activation(VO[0:L, 0:512], VO[0:L, 0:512], AF.Exp, scale=-1.0), reads=[bVO], writes=[bVO])
            yield
            mm(M[2][0:L, 16:20], cs(C_TRII, L), g8(20, 24), r=[bCT, bG8], w=[bM[2]])
            yield
            P.op("dve", lambda e: e.tensor_copy(g8(24, 28), M[2][0:L, 16:20]), reads=[bM[2]], writes=[bG8])
            yield
            P.op("dve", lambda e: e.tensor_tensor(g8(28, 32), g8(16, 20), g8(24, 28), ALU.subtract), reads=[bG8], writes=[bG8])
            yield
            idb = lambda: CT[0:L, C_ID:C_ID + L].unsqueeze(1).to_broadcast([L, 4, L])
            s4 = lambda i: SQ[i][0:L, :].rearrange("p (h t) -> p h t", h=4)[:, :, 0:L]
            p4 = lambda i: M[i][0:L, :].rearrange("p (h t) -> p h t", h=4)[:, :, 0:L]
            P.op("dve", lambda e: e.tensor_tensor(s4(0), idb(), g8(28, 32).unsqueeze(2).to_broadcast([L, 4, L]), ALU.mult), reads=[bCT, bG8], writes=[bSQ[0]])
            yield
            for h in range(4):
                mm(M[3][0:L, h * 128:h * 128 + L], cs(C_ONES, L), SQ[0][0:L, h * 128:h * 128 + L], r=[bCT, bSQ[0]], w=[bM[3]], inc=(h == 3))
            yield
            P.op("dve", lambda e: e.tensor_tensor(s4(1), p4(3), g8(24, 28).unsqueeze(2).to_broadcast([L, 4, L]), ALU.add), reads=[bM[3], bG8], writes=[bSQ[1]])
            yield
            P.op("dve", lambda e: e.tensor_tensor(s4(1), s4(1), CT[0:L, C_MNEG:C_MNEG + L].unsqueeze(1).to_broadcast([L, 4, L]), ALU.add), reads=[bSQ[1], bCT], writes=[bSQ[1]])
            yield
            P.op("dve", lambda e: e.tensor_reduce(g8(32, 36), s4(1), AX.X, ALU.max), reads=[bSQ[1]], writes=[bG8])
            m0 = M0[l][seq]
            P.op("dve", lambda e: e.tensor_tensor(g8(36, 40), g8(24, 28), m0[0:L, :], ALU.add), reads=[bG8, bM0[l][seq]], writes=[bG8])
            yield
            P.op("dve", lambda e: e.tensor_tensor(g8(40, 44), g8(36, 40), g8(32, 36), ALU.max), reads=[bG8], writes=[bG8])
            yield
            P.op("dve", lambda e: e.tensor_tensor(g8(44, 48), g8(36, 40), g8(40, 44), ALU.subtract), reads=[bG8], writes=[bG8])
            yield
            P.op("dve", lambda e: e.tensor_tensor(g8(48, 52), g8(24, 28), g8(40, 44), ALU.subtract), reads=[bG8], writes=[bG8])
            P.op("act", lambda e: e.activation(g8(44, 48), g8(44, 48), AF.Exp), reads=[bG8], writes=[bG8])
            yield
            P.op("dve", lambda e: e.tensor_tensor(s4(0), idb(), g8(48, 52).unsqueeze(2).to_broadcast([L, 4, L]), ALU.mult), reads=[bCT, bG8], writes=[bSQ[0]])
            P.op("act", lambda e: e.activation(g8(52, 56), g8(40, 44), AF.Exp, scale=-1.0), reads=[bG8], writes=[bG8])
            yield
            for h in range(4):
                mm(M[3][0:L, h * 128:h * 128 + L], cs(C_ONES, L), SQ[0][0:L, h * 128:h * 128 + L], r=[bCT, bSQ[0]], w=[bM[3]], inc=(h == 3))
            yield
            P.op("dve", lambda e: e.tensor_tensor(s4(2), p4(3), CT[0:L, C_MTNEG:C_MTNEG + L].unsqueeze(1).to_broadcast([L, 4, L]), ALU.add), reads=[bM[3], bCT], writes=[bSQ[2]])
            yield
            for h in range(4):
                P.op("act", lambda e, h=h: e.activation(SQ[2][0:L, h * 128:h * 128 + L], SQ[2][0:L, h * 128:h * 128 + L], AF.Exp, bias=G8[0:L, 28 + h:29 + h]),
                     reads=[bSQ[2], bG8], writes=[bSQ[2]])
            for h in range(4):
                mm(M[3][0:L, h * 128:h * 128 + L], QKT[:, 4 + h, ol], QKT[:, h, ol], r=[bQK], w=[bM[3]], inc=(h == 3))
            yield
            Cc = CE[l][seq]
            bCc = bCE[l][seq]
            P.op("act", lambda e: e.copy(CB[:], Cc[:]), reads=[bCc], writes=[bCB])
            for h in range(4):
                P.op("pe", lambda e, h=h: e.transpose(PSB[6][0:L, h * 128:(h + 1) * 128], QKT[:, 4 + h, ol], IDB[:, :]), reads=[bQK, bIDB], writes=[bM[2]], inc=(h == 3))
            yield
            P.op("dve", lambda e: e.tensor_tensor(STb[0:L, :].rearrange("p (h t) -> p h t", h=4)[:, :, 0:L], p4(3), s4(2), ALU.mult), reads=[bM[3], bSQ[2]], writes=[bSTb])
            evac(KTOK[0:L, :], PSB[6][0:L, 0:512], r=[bM[2]], w=[bKTOK])
            yield
            for h in range(4):
                pa = M[h // 2][0:L, (h % 2) * 129:(h % 2) * 129 + 129]
                pq = M[2 + h // 2][0:L, (h % 2) * 129:(h % 2) * 129 + 129]
                mm(pa, STb[0:L, h * 128:h * 128 + L], VE[0:L, h, :], r=[bSTb, bVE], w=[bM[h // 2]])
                mm(pq, QKT[:, h, ol], CB[:, h, :], r=[bQK, bCB], w=[bM[2 + h // 2]])
            yield
            for hp in range(2):
                nd = ND[0:L, 2 * hp:2 * hp + 2, :]
                P.op("dve", lambda e, hp=hp, nd=nd: e.tensor_tensor(nd, M[2 + hp][0:L, 0:258].rearrange("p (h e) -> p h e", h=2),
                                                                    G8[0:L, 44 + 2 * hp:46 + 2 * hp].unsqueeze(2).to_broadcast([L, 2, 129]), ALU.mult),
                     reads=[bM[2 + hp], bG8], writes=[bND])
            yield
            for hp in range(2):
                nd = ND[0:L, 2 * hp:2 * hp + 2, :]
                P.op("dve", lambda e, hp=hp, nd=nd: e.tensor_tensor(nd, nd, M[hp][0:L, 0:258].rearrange("p (h e) -> p h e", h=2), ALU.add),
                     reads=[bM[hp], bND], writes=[bND])
            yield
            el = CT[0:L, (C_EL128 if L == 128 else C_EL16):(C_EL128 if L == 128 else C_EL16) + 128]
            P.op("dve", lambda e: e.tensor_copy(G8[0:L, 0:4], g8(40, 44)), reads=[bG8], writes=[bG8])
            P.op("dve", lambda e: e.tensor_copy(G8[0:L, 4:8], g8(24, 28)), reads=[bG8], writes=[bG8])
            yield
            mm(M[2][:, 300:308], el, G8[0:L, 0:8], r=[bCT, bG8], w=[bM[2]])
            P.op("dve", lambda e: e.tensor_scalar(g8(56, 60), ND[0:L, :, 128], -1.0, None, ALU.mult), reads=[bND], writes=[bG8])
            yield
            P.op("dve", lambda e: e.tensor_tensor(g8(56, 60), g8(56, 60), ND[0:L, :, 128], ALU.max), reads=[bND, bG8], writes=[bG8])
            P.op("act", lambda e: e.copy(SMB[:, 0:8], M[2][:, 300:308]), reads=[bM[2]], writes=[bSMB])
            yield
            P.op("dve", lambda e: e.tensor_tensor(g8(56, 60), g8(56, 60), g8(52, 56), ALU.max), reads=[bG8], writes=[bG8])
            yield
            P.op("dve", lambda e: e.reciprocal(g8(56, 60), g8(56, 60)), reads=[bG8], writes=[bG8])
            P.op("dve", lambda e: e.tensor_tensor(SMB[:, 8:12], SMB[:, 4:8], SMB[:, 0:4], ALU.subtract), reads=[bSMB], writes=[bSMB])
            yield
            h3 = JKB[0:L, 0:512].rearrange("p (h e) -> p h e", h=4)
            P.op("dve", lambda e: e.tensor_tensor(h3, ND[0:L, :, 0:128], g8(56, 60).unsqueeze(2).to_broadcast([L, 4, 128]), ALU.mult), reads=[bND, bG8], writes=[bJKB])
            P.op("dve", lambda e: e.tensor_tensor(g8(8, 12), g8(28, 32), SMB[0:L, 8:12], ALU.add), reads=[bG8, bSMB], writes=[bG8])
            yield
            q3 = SQ[1][0:L, :].rearrange("p (h e) -> p h e", h=4)
            P.op("dve", lambda e: e.tensor_tensor(q3, h3, h3, ALU.mult), reads=[bJKB], writes=[bSQ[1]])
            P.op("act", lambda e: e.activation(g8(8, 12), g8(8, 12), AF.Exp), reads=[bG8], writes=[bG8])
            P.op("dve", lambda e: e.tensor_tensor(SMB[:, 12:16], SMB[:, 8:12], m0[:, :], ALU.add), reads=[bSMB, bM0[l][seq]], writes=[bSMB])
            yield
            P.op("dve", lambda e: e.tensor_reduce(g8(60, 64), q3, AX.X, ALU.add), reads=[bSQ[1]], writes=[bG8])
            P.op("act", lambda e: e.activation(SMB[:, 12:16], SMB[:, 12:16], AF.Exp), reads=[bSMB], writes=[bSMB])
            yield
            P.op("dve", lambda e: e.tensor_tensor(KW[0:L, :].rearrange("p (h d) -> p h d", h=4), KTOK[0:L, :].rearrange("p (h d) -> p h d", h=4),
                                                  g8(8, 12).unsqueeze(2).to_broadcast([L, 4, 128]), ALU.mult), reads=[bKTOK, bG8], writes=[bKW])
            P.op("act", lambda e: e.activation(g8(60, 64), g8(60, 64), AF.Ln, bias=1e-6, scale=1.0 / 128), reads=[bG8], writes=[bG8])
            yield
            P.op("dve", lambda e: e.tensor_copy(m0[:, :], SMB[:, 0:4]), reads=[bSMB], writes=[bM0[l][seq]])
            P.op("act", lambda e: e.activation(g8(60, 64), g8(60, 64), AF.Exp, scale=-0.5), reads=[bG8], writes=[bG8])
            for h in range(4):
                mm(M[h // 2][:, (h % 2) * 129:(h % 2) * 129 + 129], KW[0:L, h * 128:(h + 1) * 128], VE[0:L, h, :], r=[bKW, bVE], w=[bM[h // 2]])
            yield
            P.op("dve", lambda e: e.tensor_tensor(Cc[:], Cc[:], SMB[:, 12:16].unsqueeze(2).to_broadcast([128, 4, 129]), ALU.mult), reads=[bCc, bSMB], writes=[bCc])
            yield
            P.op("dve", lambda e: e.tensor_tensor(h3, h3, g8(60, 64).unsqueeze(2).to_broadcast([L, 4, 128]), ALU.mult), reads=[bJKB, bG8], writes=[bJKB])
            yield
            for hp in range(2):
                P.op("dve", lambda e, hp=hp: e.tensor_tensor(Cc[:, 2 * hp:2 * hp + 2, :], Cc[:, 2 * hp:2 * hp + 2, :],
                                                             M[hp][:, 0:258].rearrange("p (h e) -> p h e", h=2), ALU.add),
                     reads=[bM[hp], bCc], writes=[bCc])
            yield
            P.op("dve", lambda e: e.tensor_tensor(JKB[0:L, 0:512], JKB[0:L, 0:512], pr["hw"][0:L, :], ALU.mult), reads=[bJKB, bLW], writes=[bJKB])
            yield
            while not guard():
                yield
            P.op("dve", lambda e: e.tensor_tensor(YBb[0:L, 512:1024], JKB[0:L, 0:512], VO[0:L, 0:512], ALU.mult), reads=[bJKB, bVO], writes=[bYA])
            yield

        SMB = sb("SMB", [128, 16]); bSMB = Buf("SMB")

        for sti in range(NST + 1):
            cur["sti"] = sti
            process_st(sti)
        P.finish("sp")
        LAST["count"] = dict(P.count)
        LAST["n"] = {e: len(v) for e, v in P.streams.items()}
        P.emit()
    return nc


_WNAMES = ["norm_mix", "w_in", "a_mu", "a_w0", "a_w2", "a_a0", "a_a2", "a_g2", "a_k_k", "a_k_a", "a_r_k", "a_ln_w", "a_ln_b",
           "b_conv_w", "b_conv_b", "b_i_bias", "b_f_bias", "b_hn_w", "w_out", "norm_ffn", "w_up", "ffn_conv_w", "ffn_conv_b",
           "w_down", "norm_final"]


def run(inputs, NST, NCH=2, ncores=8):
    f = lambda a: np.ascontiguousarray(np.asarray(a, dtype=np.float32))
    nc = build_program(NST, NCH)
    cst = make_consts()
    nb = inputs["x_prompt"].shape[0]
    in_maps = []
    for c in range(ncores):
        b = c % nb
        s0 = (2 * c) % inputs["x_sample"].shape[0]
        m = {"xp": f(inputs["x_prompt"][b]), "meta": f(inputs["meta_tokens"]), "xs": f(inputs["x_sample"][s0:s0 + 2]),
             "st_shift": f(inputs["state_rwkv_shift"][:, s0:s0 + 2]), "st_wkv": f(inputs["state_rwkv_wkv"][:, s0:s0 + 2]),
             "st_conv": f(inputs["state_mlstm_conv"][:, s0:s0 + 2]), "st_C": f(inputs["state_mlstm_C"][:, s0:s0 + 2]),
             "st_n": f(inputs["state_mlstm_n"][:, s0:s0 + 2]), "st_m": f(inputs["state_mlstm_m"][:, s0:s0 + 2]),
             "st_fconv": f(inputs["state_ffn_conv"][:, s0:s0 + 2]), "consts": cst}
        for n in _WNAMES:
            m[n] = f(inputs[n])
        in_maps.append(m)
    res = run_bass_kernel_spmd(nc, in_maps, core_ids=list(range(ncores)))
    return res.results


def assemble(rs, nb, nsb):
    ncores = len(rs)
    y_prompt = np.stack([rs[b]["y_p"] for b in range(nb)])
    y_sample = np.concatenate([rs[c]["y_s"] for c in range(nsb // 2)], axis=0)
    outs = [y_prompt, y_sample]
    keys = ["o_shift", "o_wkv", "o_conv", "o_C", "o_n", "o_m", "o_fconv"]
    for k in keys:
        outs.append(np.stack([rs[b][k][:, 0] for b in range(nb)], axis=1))
    for k in keys:
        outs.append(np.concatenate([rs[c][k][:, 1:3] for c in range(nsb // 2)], axis=1))
    return tuple(np.ascontiguousarray(o, dtype=np.float32) for o in outs)


def kernel(**inputs):
    rs = run(inputs, NST=4096 // 256, NCH=2, ncores=8)
    return assemble(rs, 4, 16)
```
